# Optimizing a Trainium2 kernel written in Bass

```python
import jax, jax.numpy as jnp
from jax import lax
import numpy as np

D_MODEL = 2048
BATCH = 2
SEQ = 16384
DEPTH = 1

CTX_LEN = 256
GRID_W = 64
MIX_WIDTH = D_MODEL
HG_WIDTH = D_MODEL // 2
HG_DK = 128
HG_DV = 128
HG_HEADS = HG_WIDTH // HG_DK
CONV_WIDTH = MIX_WIDTH - HG_WIDTH
CONV_K = 3
CHUNK = 64
D_FF = ((8 * D_MODEL // 3 + 255) // 256) * 256
N_IN = 5 * HG_WIDTH + 3 * CONV_WIDTH
ALPHA = (2.0 * DEPTH) ** 0.25
BETA = (8.0 * DEPTH) ** -0.25
LN_EPS = 1e-6
RMS_EPS = 1e-6

kernel_name = "hymba_hgrn2_shortconv_dit_layer"


def layer_norm(x, gain=None, bias=None):
    xf = x.astype(jnp.float32)
    mu = jnp.mean(xf, axis=-1, keepdims=True)
    var = jnp.mean(jnp.square(xf - mu), axis=-1, keepdims=True)
    y = (xf - mu) * lax.rsqrt(var + LN_EPS)
    if gain is not None:
        y = y * gain.astype(jnp.float32) + bias.astype(jnp.float32)
    return y.astype(x.dtype)


def modulate(x, shift, scale):
    return layer_norm(x) * (1 + scale) + shift


def rms_norm(x, w):
    xf = x.astype(jnp.float32)
    return xf * lax.rsqrt(jnp.mean(jnp.square(xf), axis=-1, keepdims=True) + RMS_EPS) * w.astype(jnp.float32)


def lower_bounds(lb_logits, l):
    p = jax.nn.softmax(lb_logits.astype(jnp.float32), axis=1)
    return jnp.cumsum(p, axis=1)[:, l]


def to_dirs(a, n_heads, d):
    B, N, _ = a.shape
    return jnp.stack([a, a[:, ::-1]]).reshape(2 * B, N, n_heads, d)


def hgrn2_kv(p3, lb):
    B, N, _ = p3.shape
    f_raw = p3[..., :2 * HG_WIDTH].astype(jnp.float32)
    f_fwd = lb[0] + (1 - lb[0]) * jax.nn.sigmoid(f_raw[..., :HG_WIDTH])
    f_bwd = lb[1] + (1 - lb[1]) * jax.nn.sigmoid(f_raw[..., HG_WIDTH:])
    f = jnp.stack([f_fwd, f_bwd[:, ::-1]]).reshape(2 * B, N, HG_HEADS, HG_DK)
    v = to_dirs(p3[..., 2 * HG_WIDTH:].astype(jnp.float32), HG_HEADS, HG_DV)
    return 1 - f, jnp.log(f), v


def chunked(a):
    G, N, H, d = a.shape
    return a.reshape(G, N // CHUNK, CHUNK, H, d).transpose(1, 0, 3, 2, 4)


def hgrn2_final_state(k, logf, v):
    G, _, H, _ = k.shape
    S0 = jnp.zeros((G, H, HG_DK, HG_DV), jnp.float32)

    def step(S, inp):
        kc, gc, vc = inp
        b = jnp.cumsum(gc, axis=2)
        b_end = b[:, :, -1:, :]
        S = jnp.swapaxes(jnp.exp(b_end), -1, -2) * S + jnp.einsum('ghsk,ghsv->ghkv', kc * jnp.exp(b_end - b), vc)
        return S, None

    S, _ = lax.scan(step, S0, (chunked(k), chunked(logf), chunked(v)))
    return S


def hgrn2_scan(q, k, logf, v, S0):
    G, N, H, _ = q.shape
    tril = jnp.tril(jnp.ones((CHUNK, CHUNK), dtype=bool))

    def step(S, inp):
        qc, kc, gc, vc = inp
        b = jnp.cumsum(gc, axis=2)
        diff = b[:, :, :, None, :] - b[:, :, None, :, :]
        decay = jnp.exp(jnp.where(tril[:, :, None], diff, -jnp.inf))
        scores = jnp.einsum('ghtk,ghsk,ghtsk->ghts', qc, kc, decay)
        o = jnp.einsum('ghts,ghsv->ghtv', scores, vc) + jnp.einsum('ghtk,ghkv->ghtv', qc * jnp.exp(b), S)
        b_end = b[:, :, -1:, :]
        S = jnp.swapaxes(jnp.exp(b_end), -1, -2) * S + jnp.einsum('ghsk,ghsv->ghkv', kc * jnp.exp(b_end - b), vc)
        return S, o

    S, o = lax.scan(step, S0, (chunked(q), chunked(k), chunked(logf), chunked(v)))
    o = o.transpose(1, 0, 3, 2, 4).reshape(G, N, H, HG_DV)
    return o, S


def dwconv3(z, w, n_segments):
    B, N, C = z.shape
    L = N // n_segments
    zp = jnp.pad(z.reshape(B, n_segments, L, C), ((0, 0), (0, 0), (1, 1), (0, 0)))
    y = w[0] * zp[:, :, :L] + w[1] * zp[:, :, 1:L + 1] + w[2] * zp[:, :, 2:]
    return y.reshape(B, N, C)


def mixer(proj, S0, lb, g_norm_w, conv_w, n_segments):
    B, N, _ = proj.shape
    k_d, logf_d, v_d = hgrn2_kv(proj[..., :3 * HG_WIDTH], lb)
    q_d = to_dirs(jax.nn.silu(proj[..., 3 * HG_WIDTH:4 * HG_WIDTH].astype(jnp.float32)), HG_HEADS, HG_DK)
    o_d, S = hgrn2_scan(q_d, k_d, logf_d, v_d, S0)
    o_d = o_d.reshape(2, B, N, HG_HEADS, HG_DV)
    o = o_d[0] + o_d[1][:, ::-1]
    gate = jax.nn.silu(proj[..., 4 * HG_WIDTH:5 * HG_WIDTH].astype(jnp.float32))
    y_hg = (rms_norm(o, g_norm_w).reshape(B, N, HG_WIDTH) * gate).astype(proj.dtype)
    b_gate, c_gate, xv = jnp.split(proj[..., 5 * HG_WIDTH:], 3, axis=-1)
    y_conv = b_gate * dwconv3(c_gate * xv, conv_w, n_segments)
    return jnp.concatenate([y_hg, y_conv], axis=-1), S


def swiglu(h, w_gate, w_up, w_down):
    return (jax.nn.silu(h @ w_gate) * (h @ w_up)) @ w_down


def trunk_layer(x, xc, c, c_ctx, n_rows, w_mod, b_mod, w_in, lb, g_norm_w, conv_w, w_out,
                ln1_g, ln1_b, w_gate, w_up, w_down, ln2_g, ln2_b, update_ctx):
    mod = jax.nn.silu(c) @ w_mod + b_mod
    mod_c = jax.nn.silu(c_ctx) @ w_mod + b_mod
    sh_a, sc_a, ga_a, sh_f, sc_f, ga_f = jnp.split(mod[:, None, :], 6, axis=-1)
    shc_a, scc_a, gac_a, shc_f, scc_f, gac_f = jnp.split(mod_c, 6, axis=-1)

    hc = modulate(xc, shc_a, scc_a)
    if update_ctx:
        B = xc.shape[0]
        zero_state = jnp.zeros((2 * B, HG_HEADS, HG_DK, HG_DV), jnp.float32)
        yc, S_ctx = mixer(hc @ w_in, zero_state, lb, g_norm_w, conv_w, 1)
        xc = layer_norm(ALPHA * xc + gac_a * (yc @ w_out), ln1_g, ln1_b)
        hc = modulate(xc, shc_f, scc_f)
        xc = layer_norm(ALPHA * xc + gac_f * swiglu(hc, w_gate, w_up, w_down), ln2_g, ln2_b)
    else:
        S_ctx = hgrn2_final_state(*hgrn2_kv(hc @ w_in[:, :3 * HG_WIDTH], lb))

    h = modulate(x, sh_a, sc_a)
    y, _ = mixer(h @ w_in, S_ctx, lb, g_norm_w, conv_w, n_rows)
    x = layer_norm(ALPHA * x + ga_a * (y @ w_out), ln1_g, ln1_b)
    h = modulate(x, sh_f, sc_f)
    x = layer_norm(ALPHA * x + ga_f * swiglu(h, w_gate, w_up, w_down), ln2_g, ln2_b)
    return x, xc


def setup_inputs(seed: int = 0) -> dict:
    key = jax.random.key(seed)
    ks = jax.random.split(key, 20)
    D = D_MODEL
    nrm = jax.random.normal
    return {
        "x": nrm(ks[0], (BATCH, SEQ, D), jnp.float32),
        "c": nrm(ks[1], (BATCH, D), jnp.float32),
        "ctx": nrm(ks[2], (BATCH, CTX_LEN, D), jnp.float32),
        "c_ctx": nrm(ks[3], (D,), jnp.float32),
        "w_mod": nrm(ks[4], (DEPTH, D, 6 * D), jnp.float32) * (0.5 * D ** -0.5),
        "b_mod": nrm(ks[5], (DEPTH, 6 * D), jnp.float32) * 0.01,
        "w_in": nrm(ks[6], (DEPTH, D, N_IN), jnp.float32) * D ** -0.5,
        "lb_logits": nrm(ks[7], (2, DEPTH + 1, HG_WIDTH), jnp.float32) * 0.5,
        "g_norm_w": 1.0 + 0.01 * nrm(ks[8], (DEPTH, HG_DV), jnp.float32),
        "conv_w": nrm(ks[9], (DEPTH, CONV_K, CONV_WIDTH), jnp.float32) * CONV_K ** -0.5,
        "w_out": nrm(ks[10], (DEPTH, MIX_WIDTH, D), jnp.float32) * (BETA * MIX_WIDTH ** -0.5),
        "ln1_g": 1.0 + 0.01 * nrm(ks[11], (DEPTH, D), jnp.float32),
        "ln1_b": 0.01 * nrm(ks[12], (DEPTH, D), jnp.float32),
        "w_gate": nrm(ks[13], (DEPTH, D, D_FF), jnp.float32) * D ** -0.5,
        "w_up": nrm(ks[14], (DEPTH, D, D_FF), jnp.float32) * D ** -0.5,
        "w_down": nrm(ks[15], (DEPTH, D_FF, D), jnp.float32) * (BETA * D_FF ** -0.5),
        "ln2_g": 1.0 + 0.01 * nrm(ks[16], (DEPTH, D), jnp.float32),
        "ln2_b": 0.01 * nrm(ks[17], (DEPTH, D), jnp.float32),
    }


def reference(x, c, ctx, c_ctx, w_mod, b_mod, w_in, lb_logits, g_norm_w, conv_w, w_out,
              ln1_g, ln1_b, w_gate, w_up, w_down, ln2_g, ln2_b):
    n_rows = x.shape[1] // GRID_W
    xc = ctx
    for l in range(DEPTH):
        lb = lower_bounds(lb_logits, l)
        x, xc = trunk_layer(x, xc, c, c_ctx, n_rows, w_mod[l], b_mod[l], w_in[l], lb, g_norm_w[l],
                            conv_w[l], w_out[l], ln1_g[l], ln1_b[l], w_gate[l], w_up[l], w_down[l],
                            ln2_g[l], ln2_b[l], l < DEPTH - 1)
    return x
```

```python
import numpy as np
import concourse.bass as bass
import concourse.mybir as mybir
from concourse.bass_utils import run_bass_kernel_spmd
from contextlib import ExitStack

F32 = mybir.dt.float32
BF16 = mybir.dt.bfloat16
AF = mybir.ActivationFunctionType
ALU = mybir.AluOpType

D = 2048
KC = 16
NH = 8
DFF = 5632
JF = 44
T = 512
CTX = 256
ALPHA = 2.0 ** 0.25
EPS = 1e-6


class View:
    __slots__ = ("b", "ap")

    def __init__(self, b, ap):
        self.b = b
        self.ap = ap

    def __getitem__(self, idx):
        return View(self.b, self.ap[idx])

    def r(self, pat, **kw):
        return View(self.b, self.ap.rearrange(pat, **kw))

    def bc(self, shape):
        return View(self.b, self.ap.broadcast_to(list(shape)))

    def un(self, d):
        return View(self.b, self.ap.unsqueeze(d))


class Buf:
    __slots__ = ("t", "w", "r", "name", "dcnt", "kind")

    def __init__(self, t, name, kind):
        self.t = t
        self.name = name
        self.kind = kind
        self.w = {}
        self.r = {}
        self.dcnt = 0

    def __getitem__(self, idx):
        return View(self, self.t[idx])

    @property
    def v(self):
        return View(self, self.t[:])


class KB:
    def __init__(self, nc, es):
        self.nc = nc
        self.es = es
        self.eng = {"pe": nc.tensor, "act": nc.scalar, "dve": nc.vector,
                    "pool": nc.gpsimd, "sp": nc.sync}
        self.sems = {}
        self.cnt = {}
        self.waited = {}
        self.keycnt = {}
        for e in self.eng:
            self.sems[e] = es.enter_context(nc.semaphore("s_" + e))
            self.cnt[e] = 0
            self.waited[e] = {}
        self.ninst = 0
        self.uid = 0
        self.names = {}

    def sbuf(self, es, name, shape, dt):
        self.uid += 1
        t = es.enter_context(self.nc.sbuf_tensor("%s_%d" % (name, self.uid), list(shape), dt))
        self.names[name] = "%s_%d" % (name, self.uid)
        return Buf(t, name, "sb")

    def psum(self, es, name, shape, dt):
        t = es.enter_context(self.nc.psum_tensor(name, list(shape), dt))
        return Buf(t, name, "ps")

    def dram(self, name, shape, dt, kind="Internal"):
        t = self.nc.dram_tensor(name, list(shape), dt, kind=kind)
        return Buf(t.ap(), name, "dr")

    def _key(self, b, q="sp"):
        k = "d_" + b.name + ("_sw" if q == "pool" else "")
        if k not in self.sems:
            self.sems[k] = self.es.enter_context(self.nc.semaphore(k))
            self.keycnt[k] = 0
        return k

    def _waits(self, e, reads, writes):
        need = {}
        for b in reads:
            for k, v in b.w.items():
                if need.get(k, 0) < v:
                    need[k] = v
        for b in writes:
            for k, v in b.w.items():
                if need.get(k, 0) < v:
                    need[k] = v
            for k, v in b.r.items():
                if need.get(k, 0) < v:
                    need[k] = v
        wd = self.waited[e]
        for k, v in need.items():
            if e == "pe" and k == "pe":
                continue
            if wd.get(k, 0) < v:
                self.eng[e].wait_ge(self.sems[k], v)
                wd[k] = v

    def op(self, e, fn, reads=(), writes=(), quiet=False):
        if _DBG.get("halt"):
            return None
        ps_r = [b for b in reads if b.kind == "ps"]
        if ps_r:
            writes = list(writes) + ps_r
        self._waits(e, reads, writes)
        inst = fn(self.eng[e])
        self.ninst += 1
        if quiet:
            tok = self.cnt[e] + 1
        else:
            self.cnt[e] += 1
            tok = self.cnt[e]
            inst.then_inc(self.sems[e], 1)
        for b in reads:
            if b.r.get(e, 0) < tok:
                b.r[e] = tok
        for b in writes:
            if b.w.get(e, 0) < tok:
                b.w[e] = tok
        return inst

    def dma(self, q, out, in_, key=None):
        if _DBG.get("halt"):
            return
        src, dst = in_.b, out.b
        if key is None:
            key = dst if dst.kind == "sb" else (src if src.kind == "sb" else dst)
        self._waits(q, [src], [dst] if dst.kind != "dr" else [])
        k = self._key(key, q)
        inst = self.eng[q].dma_start(out=out.ap, in_=in_.ap)
        self.keycnt[k] += 16
        inst.then_inc(self.sems[k], 16)
        self.ninst += 1
        v = self.keycnt[k]
        if src.kind != "dr":
            src.r[k] = max(src.r.get(k, 0), v)
        dst.w[k] = max(dst.w.get(k, 0), v)

    def barrier(self):
        for e in self.eng:
            wd = self.waited[e]
            for k in list(self.sems.keys()):
                v = self.cnt[k] if k in self.cnt else self.keycnt[k]
                if k == e or v == 0:
                    continue
                if wd.get(k, 0) < v:
                    self.eng[e].wait_ge(self.sems[k], v)
                    wd[k] = v

    @staticmethod
    def _bufs(*vs):
        return [v.b for v in vs if isinstance(v, View)]

    @staticmethod
    def _a(v):
        return v.ap if isinstance(v, View) else v

    def mm(self, out, lhsT, rhs, start, stop, signal=False):
        self.op("pe", lambda e: e.matmul(out.ap, lhsT=lhsT.ap, rhs=rhs.ap, start=start, stop=stop),
                reads=[lhsT.b, rhs.b], writes=[out.b], quiet=not (stop or signal))

    def tr(self, out, in_, ident):
        self.op("pe", lambda e: e.transpose(out=out.ap, in_=in_.ap, identity=ident.ap),
                reads=[in_.b, ident.b], writes=[out.b])

    def act(self, out, in_, func, scale=1.0, bias=0.0, accum=None):
        if isinstance(scale, View) and not isinstance(bias, View):
            bias = self.cst[:, 0:1] if bias == 0.0 else self.cst[:, 1:2]
            if func == AF.Copy:
                func = AF.Identity
        rd = self._bufs(in_, scale, bias)
        wr = [out.b] + ([accum.b] if accum is not None else [])
        kw = {}
        if accum is not None:
            kw["accum_out"] = accum.ap
        self.op("act", lambda e: e.activation(out=out.ap, in_=in_.ap, func=func, scale=self._a(scale),
                                              bias=self._a(bias), **kw), reads=rd, writes=wr)

    def tt(self, eng, out, in0, in1, op):
        self.op(eng, lambda e: e.tensor_tensor(out=out.ap, in0=in0.ap, in1=in1.ap, op=op),
                reads=[in0.b, in1.b], writes=[out.b])

    def ts(self, eng, out, in0, s1, op0, s2=None, op1=None):
        rd = self._bufs(in0, s1, s2)
        if op1 is None:
            self.op(eng, lambda e: e.tensor_scalar(out=out.ap, in0=in0.ap, scalar1=self._a(s1), scalar2=None,
                                                   op0=op0), reads=rd, writes=[out.b])
        else:
            self.op(eng, lambda e: e.tensor_scalar(out=out.ap, in0=in0.ap, scalar1=self._a(s1),
                                                   scalar2=self._a(s2), op0=op0, op1=op1), reads=rd, writes=[out.b])

    def stt(self, out, in0, scalar, in1, op0, op1):
        rd = self._bufs(in0, scalar, in1)
        self.op("dve", lambda e: e.scalar_tensor_tensor(out=out.ap, in0=in0.ap, scalar=self._a(scalar),
                                                        in1=in1.ap, op0=op0, op1=op1), reads=rd, writes=[out.b])

    def copy(self, eng, out, in_):
        if eng == "act":
            self.act(out, in_, AF.Copy)
        else:
            self.op(eng, lambda e: e.tensor_copy(out=out.ap, in_=in_.ap), reads=[in_.b], writes=[out.b])

    def memset(self, eng, out, val):
        self.op(eng, lambda e: e.memset(out.ap, val), writes=[out.b])


_DBG = {}


class StopBuild(Exception):
    pass


def chk(n):
    if _DBG.get("stop") == n:
        _DBG["halt"] = True

def build_program(NT):
    _DBG["halt"] = False
    nc = bass.Bass("TRN2", target_bir_lowering=False)
    NTOK = NT * T
    ges = ExitStack()
    with ges:
        kb = KB(nc, ges)

        def ext_in(name, shape, dt=F32):
            if _DBG.get("noweights") and name in ("w_mod", "w_in", "w_out", "w_gate", "w_up", "w_down"):
                return kb.dram(name, shape, dt)
            return Buf(nc.dram_tensor(name, list(shape), dt, kind="ExternalInput").ap(), name, "dr")

        x_d = ext_in("x", [NTOK, D])
        xhf_d = ext_in("xhf", [T, D])
        xhb_d = ext_in("xhb", [T, D])
        vm_d = ext_in("vm", [128, 8])
        sel_d = ext_in("sel", [128, 2])
        cvec_d = ext_in("cvec", [128, KC, 2])
        wmod_d = ext_in("w_mod", [D, 6 * D])
        bmodfm_d = ext_in("bmod_fm", [128, 96])
        bmod_d = ext_in("b_mod", [1, 6 * D])
        win_d = ext_in("w_in", [D, 8192])
        lbl_d = ext_in("lbl", [128, 2, 2, NH])
        gw_d = ext_in("g_norm_w", [1, 128])
        cw_d = ext_in("convw", [128, 3, NH])
        wout_d = ext_in("w_out", [D, D])
        ln1g_d = ext_in("ln1_g", [1, D])
        ln1b_d = ext_in("ln1_b", [1, D])
        wg_d = ext_in("w_gate", [D, DFF])
        wu_d = ext_in("w_up", [D, DFF])
        wd_d = ext_in("w_down", [DFF, D])
        ln2g_d = ext_in("ln2_g", [1, D])
        ln2b_d = ext_in("ln2_b", [1, D])
        out_d = Buf(nc.dram_tensor("out", [NTOK, D], F32, kind="ExternalOutput").ap(), "out", "dr")

        wfm = kb.dram("wfm", [16, 128, KC, 384], BF16)
        wtm = kb.dram("wtm", [NH, 128, KC, 256], BF16)
        wo = kb.dram("wo", [4, 128, KC, 512], BF16)
        wgu = kb.dram("wgu", [22, 128, KC, 512], BF16)
        wdn = kb.dram("wdn", [4, 128, JF, 512], BF16)
        s_v = kb.dram("s_v", [NT, NH, 128, 512], BF16)
        s_sg = kb.dram("s_sg", [NT, NH, 128, 512], F32)
        s_of = kb.dram("s_of", [NT, NH, 128, 512], F32)
        s_qb = kb.dram("s_qb", [NT, NH, 128, 768], BF16)
        s_kb = kb.dram("s_kb", [NT, NH, 128, 512], BF16)
        s_eb = kb.dram("s_eb", [NT, NH, 128, 8], F32)
        s_yc = kb.dram("s_yc", [NT, NH, 128, 512], BF16)
        s_x1 = kb.dram("s_x1", [NT, 4, 128, D], F32)
        s_h2 = kb.dram("s_h2", [NT, 128, KC, 512], BF16)

        pb = [kb.psum(ges, "pb%d" % i, [128, 512], F32) for i in range(8)]
        rot = {"i": 0}

        def bank(n=4, base=0):
            i = rot["i"] % n
            rot["i"] = (i + 1) % n
            return pb[base + i]

        def bfv(b):
            return View(b, b.t[:].bitcast(BF16))

        ident = kb.sbuf(ges, "ident", [128, 128], BF16)
        identf = kb.sbuf(ges, "identf", [128, 128], F32)
        maskF = kb.sbuf(ges, "maskF", [128, 128], F32)
        maskB = kb.sbuf(ges, "maskB", [128, 128], F32)
        rmask = kb.sbuf(ges, "rmask", [128, T], F32)
        modv = kb.sbuf(ges, "modv", [128, 12, KC], F32)
        oml = kb.sbuf(ges, "oml", [128, 2, 2, NH], F32)
        gwrep = kb.sbuf(ges, "gwrep", [128, 128], F32)
        convw = kb.sbuf(ges, "convw", [128, 3, NH], F32)
        vm = kb.sbuf(ges, "vm", [128, 8], F32)
        sel = kb.sbuf(ges, "sel", [128, 2], F32)
        sstart = kb.sbuf(ges, "sstart", [128, 2, NH, 128], F32)
        ring = [kb.sbuf(ges, "ring%d" % i, [128, 8192], BF16) for i in range(3)]
        M_SCLA, M_SHFA, M_SCLF, M_SHFF, M_SCLC, M_SHFC, M_SCLHF, M_SHFHF, M_SCLHB, M_SHFHB = range(10)

        cst = kb.sbuf(ges, "cst", [128, 2], F32)
        kb.memset("dve", cst[:, 0:1], 0.0)
        kb.memset("dve", cst[:, 1:2], 1.0)
        kb.cst = cst
        kb.memset("pool", identf.v, 1.0)
        kb.op("pool", lambda e: e.affine_select(out=identf.t[:], in_=identf.t[:], pattern=[[-1, 128]],
                                                compare_op=ALU.is_equal, fill=0.0, base=0, channel_multiplier=1),
              reads=[identf], writes=[identf])
        kb.copy("dve", ident.v, identf.v)
        kb.memset("pool", maskF.v, 1.0)
        kb.op("pool", lambda e: e.affine_select(out=maskF.t[:], in_=maskF.t[:], pattern=[[1, 128]],
                                                compare_op=ALU.is_ge, fill=0.0, base=0, channel_multiplier=-1),
              reads=[maskF], writes=[maskF])
        kb.memset("pool", maskF[0:64, 64:128], 0.0)
        kb.memset("pool", maskB.v, 1.0)
        kb.op("pool", lambda e: e.affine_select(out=maskB.t[:], in_=maskB.t[:], pattern=[[-1, 128]],
                                                compare_op=ALU.is_ge, fill=0.0, base=0, channel_multiplier=1),
              reads=[maskB], writes=[maskB])
        kb.memset("pool", maskB[64:128, 0:64], 0.0)
        kb.memset("dve", rmask.v, 1.0)
        kb.memset("dve", rmask.v.r("p (c k) -> p c k", k=64)[:, :, 0:1], 0.0)
        kb.dma("sp", gwrep.v, gw_d[0:1, :].bc([128, 128]))
        kb.dma("sp", convw.v, cw_d.v)
        kb.dma("sp", vm.v, vm_d.v)
        kb.dma("sp", sel.v, sel_d.v)

        with ExitStack() as ses0:
            garep = kb.sbuf(ses0, "garep", [128, 2, D], F32)
            with ExitStack() as ses:
                cvec = kb.sbuf(ses, "cvec", [128, KC, 2], F32)
                scv = kb.sbuf(ses, "scv", [128, KC, 2], F32)
                screp = kb.sbuf(ses, "screp", [128, KC, 128], F32)
                bmodfm = kb.sbuf(ses, "bmodfm", [128, 96], F32)
                modfm = kb.sbuf(ses, "modfm", [128, 96, 2], F32)
                lbl = kb.sbuf(ses, "lbl", [128, 2, 2, NH], F32)
                wmt = [kb.sbuf(ses, "wmt%d" % i, [128, KC, 512], F32) for i in range(2)]

                kb.dma("sp", cvec.v, cvec_d.v)
                kb.dma("sp", garep[:, 0, :], bmod_d[0:1, 2 * D:3 * D].bc([128, D]))
                kb.dma("sp", garep[:, 1, :], bmod_d[0:1, 5 * D:6 * D].bc([128, D]))
                kb.dma("sp", bmodfm.v, bmodfm_d.v)
                kb.dma("sp", lbl.v, lbl_d.v)
                kb.act(scv.v, cvec.v, AF.Silu)
                kb.copy("dve", screp.v, scv[:, :, 0:1].bc([128, KC, 128]))
                kb.tt("dve", lbl[:, :, 0, :], lbl[:, :, 1, :], lbl[:, :, 0, :], ALU.subtract)
                kb.act(oml[:, :, 0, :], lbl[:, :, 0, :], AF.Sigmoid)
                kb.ts("dve", oml[:, :, 1, :], oml[:, :, 0, :], -1.0, ALU.mult)

                wm_v = wmod_d.v.r("(kc p) n -> p kc n", p=128)
                for blk in range(24):
                    fam = blk // 4
                    wt_ = wmt[blk % 2]
                    kb.dma("sp", wt_.v, wm_v[:, :, blk * 512:(blk + 1) * 512])
                    if fam in (2, 5):
                        g = 0 if fam == 2 else 1
                        col = (blk % 4) * 512
                        ps = bank(8)
                        for kc in range(KC):
                            kb.mm(ps.v, screp[:, kc, :], wt_[:, kc, :], kc == 0, kc == KC - 1)
                        kb.tt("dve", garep[:, g, col:col + 512], ps.v, garep[:, g, col:col + 512], ALU.add)
                    else:
                        ps = bank(8)
                        for cc in range(4):
                            for kc in range(KC):
                                kb.mm(ps[:, cc * 2:cc * 2 + 2], wt_[:, kc, cc * 128:(cc + 1) * 128], scv[:, kc, :],
                                      kc == 0, kc == KC - 1)
                        kb.tt("dve", modfm[:, blk * 4:(blk + 1) * 4, :],
                              ps[:, 0:8].r("p (c t) -> p c t", t=2),
                              bmodfm[:, blk * 4:(blk + 1) * 4].un(2).bc([128, 4, 2]), ALU.add)

                def mfam(f, which):
                    return modfm[:, f * KC:(f + 1) * KC, which]

                kb.ts("dve", modv[:, M_SCLA, :], mfam(1, 0), 1.0, ALU.add)
                kb.copy("dve", modv[:, M_SHFA, :], mfam(0, 0))
                kb.ts("dve", modv[:, M_SCLF, :], mfam(4, 0), 1.0, ALU.add)
                kb.copy("dve", modv[:, M_SHFF, :], mfam(3, 0))
                kb.ts("dve", modv[:, M_SCLC, :], mfam(1, 1), 1.0, ALU.add)
                kb.copy("dve", modv[:, M_SHFC, :], mfam(0, 1))
                for (dst_s, dst_h, sc) in ((M_SCLHF, M_SHFHF, 0), (M_SCLHB, M_SHFHB, 1)):
                    for (dst, own, cx) in ((dst_s, M_SCLA, M_SCLC), (dst_h, M_SHFA, M_SHFC)):
                        kb.tt("dve", modv[:, dst, :], modv[:, cx, :], modv[:, own, :], ALU.subtract)
                        kb.stt(modv[:, dst, :], modv[:, dst, :], sel[:, sc:sc + 1], modv[:, own, :],
                               ALU.mult, ALU.add)
                kb.barrier()

            with ExitStack() as ses:
              if _DBG.get("stop") != 5:
                    csrc = [kb.sbuf(ses, "csrc%d" % i, [128, 8192], F32) for i in range(2)]
                    cdst = [kb.sbuf(ses, "cdst%d" % i, [128, 8192], BF16) for i in range(2)]
                    engs = ["act", "dve", "pool"]
                    blocks = []

                    for kc in range(KC):
                        def loads(s_, kc=kc):
                            return [(s_.v, win_d[kc * 128:(kc + 1) * 128, :])]

                        def casts(s_, d_, i):
                            sv = s_.v.r("p (f h c) -> p f h c", f=8, h=NH)
                            fm = d_[:, 0:6144].r("p (u g c) -> p u g c", u=16, g=3)
                            tmv = d_[:, 6144:8192].r("p (h g c) -> p h g c", h=NH, g=2)
                            plan = [(fm[:, 0:8, 0, :], 0), (fm[:, 0:8, 1, :], 1), (fm[:, 0:8, 2, :], 3),
                                    (fm[:, 8:16, 0, :], 5), (fm[:, 8:16, 1, :], 6), (fm[:, 8:16, 2, :], 7),
                                    (tmv[:, :, 0, :], 2), (tmv[:, :, 1, :], 4)]
                            for n, (dv, f) in enumerate(plan):
                                kb.copy(engs[(n + i) % 3], dv, sv[:, f, :, :])

                        def stores(d_, kc=kc):
                            return [(wfm[:, :, kc, :].r("u p c -> p u c"), d_[:, 0:6144].r("p (u c) -> p u c", u=16)),
                                    (wtm[:, :, kc, :].r("h p c -> p h c"), d_[:, 6144:8192].r("p (h c) -> p h c", h=NH))]
                        blocks.append((loads, casts, stores))
                    for kc in range(KC):
                        def loads(s_, kc=kc):
                            return [(s_[:, 0:2048], wout_d[kc * 128:(kc + 1) * 128, :])]

                        def casts(s_, d_, i):
                            for q in range(4):
                                kb.tt(["dve", "pool"][q % 2], d_[:, q * 512:(q + 1) * 512], s_[:, q * 512:(q + 1) * 512],
                                      garep[:, 0, q * 512:(q + 1) * 512], ALU.mult)

                        def stores(d_, kc=kc):
                            return [(wo[:, :, kc, :].r("g p c -> p g c"), d_[:, 0:2048].r("p (g c) -> p g c", g=4))]
                        blocks.append((loads, casts, stores))
                    for kc in range(KC):
                        for hf in range(2):
                            def loads(s_, kc=kc, hf=hf):
                                c0 = hf * 2816
                                return [(s_[:, 0:2816], wg_d[kc * 128:(kc + 1) * 128, c0:c0 + 2816]),
                                        (s_[:, 2816:5632], wu_d[kc * 128:(kc + 1) * 128, c0:c0 + 2816])]

                            def casts(s_, d_, i):
                                dv = d_[:, 0:5632].r("p (j gu c) -> p j gu c", j=22, gu=2)
                                kb.copy(engs[i % 3], dv[:, :, 0, :], s_[:, 0:2816].r("p (j c) -> p j c", j=22))
                                kb.copy(engs[(i + 1) % 3], dv[:, :, 1, :], s_[:, 2816:5632].r("p (j c) -> p j c", j=22))

                            def stores(d_, kc=kc, hf=hf):
                                return [(wgu[hf * 11:(hf + 1) * 11, :, kc, :].r("jg p c -> p jg c"),
                                         d_[:, 0:5632].r("p (jg c) -> p jg c", jg=11))]
                            blocks.append((loads, casts, stores))
                    for jb in range(JF // 4):
                        def loads(s_, jb=jb):
                            return [(s_.v.r("p (j n) -> p j n", j=4),
                                     wd_d[jb * 512:(jb + 1) * 512, :].r("(j p) n -> p j n", p=128))]

                        def casts(s_, d_, i):
                            for q in range(4):
                                kb.tt(["dve", "pool"][q % 2], d_[:, q * 2048:(q + 1) * 2048],
                                      s_[:, q * 2048:(q + 1) * 2048], garep[:, 1, :], ALU.mult)

                        def stores(d_, jb=jb):
                            dv4 = d_.v.r("p (j g c) -> p j g c", j=4, g=4)
                            return [(wdn[g, :, jb * 4:(jb + 1) * 4, :], dv4[:, :, g, :]) for g in range(4)]
                        blocks.append((loads, casts, stores))

                    def do_loads(i):
                        for (dv, sv) in blocks[i][0](csrc[i % 2]):
                            kb.dma("sp", dv, sv)
                    do_loads(0)
                    for i in range(len(blocks)):
                        if i + 1 < len(blocks):
                            do_loads(i + 1)
                        blocks[i][1](csrc[i % 2], cdst[i % 2], i)
                        for (dv, sv) in blocks[i][2](cdst[i % 2]):
                            kb.dma("pool", dv, sv)
                    kb.barrier()

        def ln_stats(xin, stat, eps_t):
            for q in range(4):
                kb.op("dve", lambda e, q=q: e.bn_stats(out=stat.t[:, 8 + q * 6:14 + q * 6],
                                                       in_=xin.ap[:, q * 512:(q + 1) * 512]),
                      reads=[xin.b], writes=[stat])
            kb.op("dve", lambda e: e.bn_aggr(out=stat.t[:, 2:4], in_=stat.t[:, 8:32]), reads=[stat], writes=[stat])
            kb.act(stat[:, 4:5], stat[:, 3:4], AF.Sqrt, bias=eps_t[:, 0:1])
            kb.op("dve", lambda e: e.reciprocal(out=stat.t[:, 0:1], in_=stat.t[:, 4:5]), reads=[stat], writes=[stat])
            kb.stt(stat[:, 1:2], stat[:, 2:3], -1.0, stat[:, 0:1], ALU.mult, ALU.mult)

        def to_fm(xn, hdst, col0, scl_i, shf_i):
            for half in range(2):
                ps = bank(4)
                pv = bfv(ps)
                for k8 in range(8):
                    kc = half * 8 + k8
                    kb.tr(pv[:, k8 * 128:(k8 + 1) * 128], xn[:, kc * 128:(kc + 1) * 128], ident.v)
                for k8 in range(8):
                    kc = half * 8 + k8
                    if k8 % 2 == 0:
                        kb.ts("dve", hdst[:, kc, col0:col0 + 128], pv[:, k8 * 128:(k8 + 1) * 128],
                              modv[:, scl_i, kc:kc + 1], ALU.mult, modv[:, shf_i, kc:kc + 1], ALU.add)
                    else:
                        kb.act(hdst[:, kc, col0:col0 + 128], pv[:, k8 * 128:(k8 + 1) * 128], AF.Identity,
                               scale=modv[:, scl_i, kc:kc + 1], bias=modv[:, shf_i, kc:kc + 1])

        class Ring:
            def __init__(self, items, hold=1):
                self.hold = hold
                self.items = items
                self.issued = 0

            def get(self, i):
                while self.issued < min(len(self.items), i + 4 - self.hold):
                    n = self.issued
                    dv, ncols, pat = self.items[n]
                    slot = ring[n % 3]
                    sv = slot[:, 0:ncols]
                    if pat is not None:
                        sv = sv.r(pat[0], **pat[1])
                    kb.dma("sp", sv, dv)
                    self.issued += 1
                return ring[i % 3]

        scan_rot = {"i": 0}

        def scan_dir(es_s, fwd, Qp, Kt, Vh, Eend, S32, SB, tmpb, O_out, add_to, bset=None):
            KT_A, KT_B, PT, sbf, tmp = tmpb["KT_A"], tmpb["KT_B"], tmpb["PT"], tmpb["sbf"], tmpb["tmp"]
            mask = maskF if fwd else maskB
            pairs = range(4) if fwd else range(3, -1, -1)
            cur = SB
            for pi, p in enumerate(pairs):
                bsel = (scan_rot["i"] % 2) if bset is None else bset
                bx = pb[4 + 2 * bsel]
                by = pb[5 + 2 * bsel]
                scan_rot["i"] += 1
                bxb = bfv(bx)
                ksl = Kt[:, p * 128:(p + 1) * 128]
                if Qp is not None:
                    qpair = Qp[:, p * 192:(p + 1) * 192].r("p (a c) -> p a c", c=64)[:, 0:3:2, :]
                    kb.mm(bx[:, 0:128], ksl, qpair, True, True)
                kb.tr(bxb[:, 512:640], ksl, ident.v)
                if Qp is not None:
                    kb.tt("dve", PT.v, bx[:, 0:128], mask.v, ALU.mult)
                kb.copy("act", KT_A[0:64, :], bxb[0:64, 512:640])
                kb.copy("act", KT_B[64:128, :], bxb[64:128, 512:640])
                yield
                order = ((KT_A, 2 * p), (KT_B, 2 * p + 1)) if fwd else ((KT_B, 2 * p + 1), (KT_A, 2 * p))
                states = [cur]
                for ci_, (KTx, ch) in enumerate(order):
                    ups = by[:, ci_ * 128:(ci_ + 1) * 128]
                    kb.mm(ups, KTx.v, Vh[:, p, :], True, True)
                    kb.tt("dve", tmp.v, ups, S32, ALU.add)
                    kb.ts("dve", S32, tmp.v, Eend[:, ch:ch + 1], ALU.mult)
                    last = (pi == 3 and ci_ == 1)
                    nxt = SB if last else sbf[(2 * pi + ci_) % 4].v
                    kb.act(nxt, tmp.v, AF.Copy, scale=Eend[:, ch:ch + 1])
                    states.append(nxt)
                    yield
                if Qp is not None:
                    ops_ = by[:, 256:384]
                    q_lo = Qp[:, p * 192:p * 192 + 128]
                    q_hi = Qp[:, p * 192 + 64:p * 192 + 192]
                    q1, q2 = (q_lo, q_hi) if fwd else (q_hi, q_lo)
                    kb.mm(ops_, PT.v, Vh[:, p, :], True, False)
                    kb.mm(ops_, q1, states[0], False, False)
                    kb.mm(ops_, q2, states[1], False, True)
                    if add_to is None:
                        kb.copy("act", O_out[:, p, :], ops_)
                    else:
                        kb.tt("dve", O_out[:, p, :], ops_, add_to[:, p, :], ALU.add)
                cur = states[2]

        def drain(g):
            if g is not None:
                for _ in g:
                    pass

        def interleave(main, filler, k=1, first=None):
            n_y = 0
            for _ in main:
                n_y += 1
                if filler is not None:
                    for _i in range(first if (first and n_y == 1) else k):
                        try:
                            next(filler)
                        except StopIteration:
                            filler = None
                            break
            return filler

        def scan_tmps(es_s):
            tb = {"KT_A": kb.sbuf(es_s, "KT_A", [128, 128], BF16), "KT_B": kb.sbuf(es_s, "KT_B", [128, 128], BF16),
                  "PT": kb.sbuf(es_s, "PT", [128, 128], BF16), "tmp": kb.sbuf(es_s, "stmp", [128, 128], F32),
                  "sbf": [kb.sbuf(es_s, "sbf%d" % i, [128, 128], BF16) for i in range(4)]}
            kb.memset("pool", tb["KT_A"].v, 0.0)
            kb.memset("pool", tb["KT_B"].v, 0.0)
            return tb

        try:
         with ExitStack() as s1:
           if _DBG.get("stop") not in (4, 5):
                 xs = [kb.sbuf(s1, "xs%d" % i, [128, D], F32) for i in range(2)]
                 xn = [kb.sbuf(s1, "xn%d" % i, [128, D], BF16) for i in range(2)]
                 stat = kb.sbuf(s1, "stat", [128, 32], F32)
                 epst = kb.sbuf(s1, "epst", [128, 1], F32)
                 hT = kb.sbuf(s1, "hT", [128, KC, T], BF16)
                 Vh = [kb.sbuf(s1, "Vh%d" % i, [128, 4, 128], BF16) for i in range(3)]
                 SGh = [kb.sbuf(s1, "SGh%d" % i, [128, 4, 128], F32) for i in range(3)]
                 Ebf = [kb.sbuf(s1, "Ebf%d" % i, [128, 8], F32) for i in range(2)]
                 Oh = [kb.sbuf(s1, "Oh%d" % i, [128, 4, 128], F32) for i in range(2)]
                 qs = kb.sbuf(s1, "qs", [128, T], F32)
                 sg_ = [kb.sbuf(s1, "sg%d" % i, [128, T], F32) for i in range(2)]
                 kk_ = [kb.sbuf(s1, "kk%d" % i, [128, T], F32) for i in range(2)]
                 lf_ = [kb.sbuf(s1, "lf%d" % i, [128, T], F32) for i in range(2)]
                 bb_ = [kb.sbuf(s1, "bb%d" % i, [128, T], F32) for i in range(2)]
                 e1_ = [kb.sbuf(s1, "e1%d" % i, [128, T], F32) for i in range(2)]
                 e2_ = [kb.sbuf(s1, "e2%d" % i, [128, T], F32) for i in range(2)]
                 Qp_ = [kb.sbuf(s1, "Qp%d" % i, [128, 768], BF16) for i in range(4)]
                 Kt_ = [kb.sbuf(s1, "Kt%d" % i, [128, T], BF16) for i in range(4)]
                 Eb_ = [kb.sbuf(s1, "Eb%d" % i, [128, 8], F32) for i in range(2)]
                 S32f = kb.sbuf(s1, "S32f", [128, NH, 128], F32)
                 SBf = kb.sbuf(s1, "SBf", [128, NH, 128], BF16)
                 S32h = kb.sbuf(s1, "S32h", [128, NH, 128], F32)
                 SBh = kb.sbuf(s1, "SBh", [128, NH, 128], BF16)
                 xv_s = kb.sbuf(s1, "xv_s", [128, T], F32)
                 zc = kb.sbuf(s1, "zc", [128, T], F32)
                 acc = kb.sbuf(s1, "acc", [128, T], F32)
                 yc = [kb.sbuf(s1, "yc%d" % i, [128, T], BF16) for i in range(2)]
                 tb = scan_tmps(s1)
                 kb.memset("dve", epst.v, EPS)
                 for q_ in Qp_:
                     kb.memset("pool", q_.v, 0.0)
                 chk(10)

                 def ln_tile(src_rows, scl_i, shf_i):
                     for sub in range(4):
                         xb_, xnb = xs[sub % 2], xn[sub % 2]
                         kb.dma("sp", xb_.v, src_rows(sub))
                         ln_stats(xb_.v, stat, epst)
                         kb.act(xnb.v, xb_.v, AF.Identity, scale=stat[:, 0:1], bias=stat[:, 1:2])
                         to_fm(xnb, hT, sub * 128, scl_i, shf_i)

                 def gates(fps, d, h, i2):
                     sg, kk, lf, bb, e1, e2 = sg_[i2], kk_[i2], lf_[i2], bb_[i2], e1_[i2], e2_[i2]
                     kb.act(sg.v, fps, AF.Sigmoid, scale=-1.0)
                     kb.ts("pool", kk.v, sg.v, oml[:, d, 0, h:h + 1], ALU.mult)
                     kb.act(lf.v, sg.v, AF.Ln, scale=oml[:, d, 1, h:h + 1], bias=1.0)
                     kb.op("dve", lambda e: e.tensor_tensor_scan(out=bb.t[:], data0=rmask.t[:], data1=lf.t[:], initial=0.0,
                                                                 op0=ALU.mult, op1=ALU.add),
                           reads=[rmask, lf], writes=[bb])
                     cum = bb
                     if d == 1:
                         kb.tt("pool", lf.v, lf.v, bb.v, ALU.subtract)
                         kb.tt("pool", sg.v.r("p (c k) -> p c k", k=64), lf.v.r("p (c k) -> p c k", k=64),
                               bb.v.r("p (c k) -> p c k", k=64)[:, :, 63:64].bc([128, 8, 64]), ALU.add)
                         cum = sg
                     kb.act(e1.v, cum.v, AF.Exp)
                     kb.act(e2.v, cum.v, AF.Exp, scale=-1.0)
                     return kk, e1, e2

                 def halo(kind):
                     d = 0 if kind == "f" else 1
                     src = xhf_d if d == 0 else xhb_d
                     ln_tile(lambda sub: src[sub * 128:(sub + 1) * 128, :],
                             M_SCLHF if d == 0 else M_SCLHB, M_SHFHF if d == 0 else M_SHFHB)
                     chk(11)
                     items = []
                     for h in range(NH):
                         items.append((wfm[h], KC * 384, ("p (k c) -> p k c", dict(k=KC))))
                         items.append((wtm[h], KC * 256, ("p (k c) -> p k c", dict(k=KC))))
                     rg = Ring(items, hold=2)
                     kb.memset("dve", S32h.v, 0.0)
                     kb.memset("pool", SBh.v, 0.0)
                     hctx = {}

                     def hproj(h):
                         wf_ = rg.get(2 * h)[:, 0:KC * 384].r("p (k c) -> p k c", k=KC)
                         wt_ = rg.get(2 * h + 1)[:, 0:KC * 256].r("p (k c) -> p k c", k=KC)
                         vh = Vh[h % 2]
                         for sub in range(4):
                             ps = bank(4)
                             for kc in range(KC):
                                 kb.mm(ps[:, 0:128], hT[:, kc, sub * 128:(sub + 1) * 128], wt_[:, kc, 0:128],
                                       kc == 0, kc == KC - 1)
                                 if kc % 8 == 7:
                                     yield
                             kb.ts("dve", vh[:, sub, :], ps[:, 0:128], vm[:, d * 4 + sub:d * 4 + sub + 1], ALU.mult)
                         ps = bank(4)
                         for kc in range(KC):
                             kb.mm(ps.v, wf_[:, kc, d * 128:(d + 1) * 128], hT[:, kc, :], kc == 0, kc == KC - 1)
                             if kc % 8 == 7:
                                 yield
                         hctx[h] = ps

                     def hprep(h):
                         kk, e1, e2 = gates(hctx[h].v, d, h, h % 2)
                         kt = Kt_[h % 4]
                         kb.tt("pool", kt.v, kk.v, e2.v, ALU.mult)
                         ee = e1.v.r("p (c k) -> p c k", k=64)
                         eend = ee[:, :, 63] if d == 0 else ee[:, :, 0]
                         kb.copy("dve", Eb_[h % 2].v, eend)

                     drain(hproj(0))
                     hprep(0)
                     for h in range(NH):
                         fil = hproj(h + 1) if h + 1 < NH else None
                         sc = scan_dir(s1, d == 0, None, Kt_[h % 4].v, Vh[h % 2].v, Eb_[h % 2].v, S32h[:, h, :],
                                       SBh[:, h, :], tb, None, None)
                         fil = interleave(sc, fil)
                         drain(fil)
                         if h + 1 < NH:
                             hprep(h + 1)
                     kb.copy("dve", sstart[:, d, :, :], S32h.v)

                 halo("f")
                 chk(12)
                 halo("b")
                 chk(13)
                 kb.copy("dve", S32f.v, sstart[:, 0, :, :])
                 kb.copy("act", SBf.v, sstart[:, 0, :, :])
                 chk(131)

                 for t in range(NT):
                     ln_tile(lambda sub: x_d[t * T + sub * 128:t * T + (sub + 1) * 128, :], M_SCLA, M_SHFA)
                     chk(132)
                     items = []
                     for h in range(NH):
                         items.append((wfm[h], KC * 384, ("p (k c) -> p k c", dict(k=KC))))
                         items.append((wtm[h], KC * 256, ("p (k c) -> p k c", dict(k=KC))))
                     for c in range(NH):
                         items.append((wfm[8 + c], KC * 384, ("p (k c) -> p k c", dict(k=KC))))
                     rg = Ring(items, hold=2)
                     pctx = {}
                     qpad = lambda b_: b_.v.r("p (q a c) -> p q a c", q=4, a=3)[:, :, 0:3:2, :]
                     q4 = lambda v_: v_.r("p (q a c) -> p q a c", q=4, a=2)

                     def proj_head(h):
                         wf_ = rg.get(2 * h)[:, 0:KC * 384].r("p (k c) -> p k c", k=KC)
                         wt_ = rg.get(2 * h + 1)[:, 0:KC * 256].r("p (k c) -> p k c", k=KC)
                         vh, sgh = Vh[h % 3], SGh[h % 3]
                         for sub in range(4):
                             ps = bank(4)
                             for kc in range(KC):
                                 kb.mm(ps[:, 0:256], hT[:, kc, sub * 128:(sub + 1) * 128], wt_[:, kc, :],
                                       kc == 0, kc == KC - 1)
                                 if kc % 8 == 7:
                                     yield
                             kb.copy("dve", vh[:, sub, :], ps[:, 0:128])
                             kb.act(sgh[:, sub, :], ps[:, 128:256], AF.Silu)
                             kb.tt("pool", sgh[:, sub, :], sgh[:, sub, :], gwrep.v, ALU.mult)
                         pf = [bank(4) for _ in range(3)]
                         for g in range(3):
                             for kc in range(KC):
                                 kb.mm(pf[g].v, wf_[:, kc, g * 128:(g + 1) * 128], hT[:, kc, :], kc == 0, kc == KC - 1)
                                 if kc % 8 == 7:
                                     yield
                         pctx[h] = pf

                     def prep(h):
                         pf = pctx[h]
                         kb.act(qs.v, pf[2].v, AF.Silu)
                         kk, e1, e2 = gates(pf[0].v, 0, h, 0)
                         qpf, ktf = Qp_[h % 2], Kt_[h % 2]
                         kb.tt("dve", qpad(qpf), q4(qs.v), q4(e1.v), ALU.mult)
                         kb.tt("pool", ktf.v, kk.v, e2.v, ALU.mult)
                         kb.copy("dve", Ebf[h % 2].v, e1.v.r("p (c k) -> p c k", k=64)[:, :, 63])
                         kk2, e1b, e2b = gates(pf[1].v, 1, h, 1)
                         qpb, ktb = Qp_[2 + h % 2], Kt_[2 + h % 2]
                         kb.tt("dve", qpad(qpb), q4(qs.v), q4(e1b.v), ALU.mult)
                         kb.tt("pool", ktb.v, kk2.v, e2b.v, ALU.mult)
                         kb.copy("dve", Eb_[1].v, e1b.v.r("p (c k) -> p c k", k=64)[:, :, 0])
                         kb.dma("pool", s_qb[t, h], qpb.v)
                         kb.dma("pool", s_kb[t, h], ktb.v)
                         kb.dma("pool", s_eb[t, h], Eb_[1].v)

                     def conv_all():
                         for c in range(NH):
                             wf_ = rg.get(2 * NH + c)[:, 0:KC * 384].r("p (k c) -> p k c", k=KC)
                             pf = [bank(4) for _ in range(3)]
                             for g in range(3):
                                 for kc in range(KC):
                                     kb.mm(pf[g].v, wf_[:, kc, g * 128:(g + 1) * 128], hT[:, kc, :],
                                           kc == 0, kc == KC - 1)
                                     if kc % 8 == 7:
                                         yield
                             kb.copy("act", xv_s.v, pf[2].v)
                             kb.tt("dve", zc.v, pf[1].v, xv_s.v, ALU.mult)
                             z3 = zc.v.r("p (s k) -> p s k", k=64)
                             a3 = acc.v.r("p (s k) -> p s k", k=64)
                             kb.act(acc.v, zc.v, AF.Copy, scale=convw[:, 1, c:c + 1])
                             kb.stt(a3[:, :, 1:64], z3[:, :, 0:63], convw[:, 0, c:c + 1], a3[:, :, 1:64],
                                    ALU.mult, ALU.add)
                             kb.stt(a3[:, :, 0:63], z3[:, :, 1:64], convw[:, 2, c:c + 1], a3[:, :, 0:63],
                                    ALU.mult, ALU.add)
                             kb.tt("dve", yc[c % 2].v, pf[0].v, acc.v, ALU.mult)
                             kb.dma("pool", s_yc[t, c], yc[c % 2].v)

                     drain(proj_head(0))
                     prep(0)
                     drain(proj_head(1))
                     cgen = conv_all()
                     for h in range(NH):
                         vh, sgh, oh = Vh[h % 3], SGh[h % 3], Oh[h % 2]
                         if h + 1 < NH:
                             prep(h + 1)
                         fil = proj_head(h + 2) if h + 2 < NH else cgen
                         sc = scan_dir(s1, True, Qp_[h % 2].v, Kt_[h % 2].v, vh.v, Ebf[h % 2].v, S32f[:, h, :],
                                       SBf[:, h, :], tb, oh.v, None)
                         fil = interleave(sc, fil, first=5)
                         kb.dma("pool", s_v[t, h], vh.v.r("p a c -> p (a c)"))
                         kb.dma("pool", s_sg[t, h], sgh.v.r("p a c -> p (a c)"))
                         kb.dma("pool", s_of[t, h], oh.v.r("p a c -> p (a c)"))
                         if h + 2 < NH:
                             drain(fil)
                     drain(cgen)
                 kb.barrier()
        except StopBuild:
            kb.barrier()

        if _DBG.get("stop", 0) in (0, 2) or _DBG.get("stop", 0) >= 20:
         with ExitStack() as s2:
          if True:
                g1rep = kb.sbuf(s2, "g1rep", [128, D], F32)
                b1rep = kb.sbuf(s2, "b1rep", [128, D], F32)
                stat = kb.sbuf(s2, "stat2", [128, 32], F32)
                epst = kb.sbuf(s2, "epst2", [128, 1], F32)
                rst = kb.sbuf(s2, "rst", [128, 16], F32)
                junk = kb.sbuf(s2, "junk", [128, 128], F32)
                S32b = kb.sbuf(s2, "S32b", [128, NH, 128], F32)
                SBb = kb.sbuf(s2, "SBb", [128, NH, 128], BF16)
                qb = [kb.sbuf(s2, "qb%d" % i, [128, 768], BF16) for i in range(2)]
                kbb = [kb.sbuf(s2, "kbb%d" % i, [128, T], BF16) for i in range(2)]
                ebb = [kb.sbuf(s2, "ebb%d" % i, [128, 8], F32) for i in range(2)]
                vb = [kb.sbuf(s2, "vb%d" % i, [128, 4, 128], BF16) for i in range(2)]
                ofb = [kb.sbuf(s2, "ofb%d" % i, [128, 4, 128], F32) for i in range(2)]
                sgb = [kb.sbuf(s2, "sgb%d" % i, [128, 4, 128], F32) for i in range(2)]
                Yt = kb.sbuf(s2, "Yt", [128, 4, 1024], BF16)
                Yf = kb.sbuf(s2, "Yf", [128, KC, T], BF16)
                R1 = [kb.sbuf(s2, "R1_%d" % i, [128, D], F32) for i in range(4)]
                xn2 = [kb.sbuf(s2, "xn2%d" % i, [128, D], BF16) for i in range(2)]
                H2 = kb.sbuf(s2, "H2", [128, KC, T], BF16)
                tb = scan_tmps(s2)
                tb2 = scan_tmps(s2)
                kb.memset("dve", epst.v, EPS)
                kb.dma("sp", g1rep.v, ln1g_d[0:1, :].bc([128, D]))
                kb.dma("sp", b1rep.v, ln1b_d[0:1, :].bc([128, D]))
                kb.copy("dve", S32b.v, sstart[:, 1, :, :])
                kb.copy("act", SBb.v, sstart[:, 1, :, :])
                for t in range(NT - 1, -1, -1):
                    for hp in range(0, NH, 2):
                        gens = []
                        for i2 in range(2):
                            h = hp + i2
                            kb.dma("sp", qb[i2].v, s_qb[t, h])
                            kb.dma("sp", kbb[i2].v, s_kb[t, h])
                            kb.dma("sp", ebb[i2].v, s_eb[t, h])
                            kb.dma("sp", vb[i2].v.r("p a c -> p (a c)"), s_v[t, h])
                            kb.dma("sp", ofb[i2].v.r("p a c -> p (a c)"), s_of[t, h])
                            kb.dma("sp", sgb[i2].v.r("p a c -> p (a c)"), s_sg[t, h])
                            gens.append(scan_dir(s2, False, qb[i2].v, kbb[i2].v, vb[i2].v, ebb[i2].v, S32b[:, h, :],
                                                 SBb[:, h, :], tb if i2 == 0 else tb2, ofb[i2].v, ofb[i2].v, bset=i2))
                        rest = interleave(gens[0], gens[1])
                        drain(rest)
                        for i2 in range(2):
                            h = hp + i2
                            for sub in range(4):
                                kb.act(junk.v, ofb[i2][:, sub, :], AF.Square, accum=rst[:, sub:sub + 1])
                            kb.act(rst[:, 4:8], rst[:, 0:4], AF.Sqrt, scale=1.0 / 128.0, bias=epst[:, 0:1])
                            kb.op("dve", lambda e: e.reciprocal(out=rst.t[:, 8:12], in_=rst.t[:, 4:8]),
                                  reads=[rst], writes=[rst])
                            for sub in range(4):
                                kb.stt(Yt[:, sub, h * 128:(h + 1) * 128], ofb[i2][:, sub, :], rst[:, 8 + sub:9 + sub],
                                       sgb[i2][:, sub, :], ALU.mult, ALU.mult)
                    for sub in range(4):
                        ps = bank(4)
                        pv = bfv(ps)
                        for h in range(NH):
                            kb.tr(pv[:, h * 128:(h + 1) * 128], Yt[:, sub, h * 128:(h + 1) * 128], ident.v)
                        kb.copy("act" if sub % 2 else "dve", Yf[:, 0:8, sub * 128:(sub + 1) * 128],
                                pv.r("p (h c) -> p h c", h=8))
                    kb.dma("sp", Yf[:, 8:16, :], s_yc[t].r("c p n -> p c n"))
                    for sub in range(4):
                        kb.dma("sp", R1[sub].v, x_d[t * T + sub * 128:t * T + (sub + 1) * 128, :])
                    rg = Ring([(wo[g], KC * 512, ("p (k c) -> p k c", dict(k=KC))) for g in range(4)])
                    for g in range(4):
                        w_ = rg.get(g)[:, 0:KC * 512].r("p (k c) -> p k c", k=KC)
                        for sub in range(4):
                            ps = bank(4)
                            for kc in range(KC):
                                kb.mm(ps.v, Yf[:, kc, sub * 128:(sub + 1) * 128], w_[:, kc, :], kc == 0, kc == KC - 1)
                            kb.stt(R1[sub][:, g * 512:(g + 1) * 512], R1[sub][:, g * 512:(g + 1) * 512], ALPHA, ps.v,
                                   ALU.mult, ALU.add)
                    for sub in range(4):
                        r1 = R1[sub].v
                        ln_stats(r1, stat, epst)
                        kb.act(r1, r1, AF.Identity, scale=stat[:, 0:1], bias=stat[:, 1:2])
                        kb.tt("dve", r1, r1, g1rep.v, ALU.mult)
                        kb.tt("pool", r1, r1, b1rep.v, ALU.add)
                        kb.dma("pool", s_x1[t, sub], r1)
                        ln_stats(r1, stat, epst)
                        kb.act(xn2[sub % 2].v, r1, AF.Identity, scale=stat[:, 0:1], bias=stat[:, 1:2])
                        to_fm(xn2[sub % 2], H2, sub * 128, M_SCLF, M_SHFF)
                    kb.dma("pool", s_h2[t], H2.v)
                kb.barrier()

        if _DBG.get("stop", 0) == 0 or _DBG.get("stop", 0) >= 30:
         with ExitStack() as s3:
          if True:
                g2rep = kb.sbuf(s3, "g2rep", [128, D], F32)
                b2rep = kb.sbuf(s3, "b2rep", [128, D], F32)
                stat = kb.sbuf(s3, "stat3", [128, 32], F32)
                epst = kb.sbuf(s3, "epst3", [128, 1], F32)
                H2b = [kb.sbuf(s3, "H2b0", [128, KC, T], BF16)]
                A = kb.sbuf(s3, "A", [128, JF, T], BF16)
                sgt = [kb.sbuf(s3, "sgt%d" % i, [128, T], F32) for i in range(2)]
                R2 = [kb.sbuf(s3, "R2_%d" % i, [128, D], F32) for i in range(4)]
                kb.memset("dve", epst.v, EPS)
                kb.dma("sp", g2rep.v, ln2g_d[0:1, :].bc([128, D]))
                kb.dma("sp", b2rep.v, ln2b_d[0:1, :].bc([128, D]))
                for t in range(NT):
                    h2 = H2b[0]
                    kb.dma("sp", h2.v, s_h2[t])
                    items = [(wgu[jg], KC * 512, ("p (k c) -> p k c", dict(k=KC))) for jg in range(22)]
                    for g in range(4):
                        for pc in range(4):
                            items.append((wdn[g][:, pc * 11:(pc + 1) * 11, :], 11 * 512, ("p (j c) -> p j c", dict(j=11))))
                    rg = Ring(items)
                    for jg in range(22):
                        w_ = rg.get(jg)[:, 0:KC * 512].r("p (k jj gu c) -> p k jj gu c", k=KC, jj=2, gu=2)
                        for jj in range(2):
                            j = jg * 2 + jj
                            pg, pu = bank(8), bank(8)
                            for kc in range(KC):
                                kb.mm(pg.v, w_[:, kc, jj, 0, :], h2[:, kc, :], kc == 0, kc == KC - 1)
                            for kc in range(KC):
                                kb.mm(pu.v, w_[:, kc, jj, 1, :], h2[:, kc, :], kc == 0, kc == KC - 1)
                            kb.act(sgt[j % 2].v, pg.v, AF.Silu)
                            kb.tt("dve", A[:, j, :], pu.v, sgt[j % 2].v, ALU.mult)
                    for sub in range(4):
                        kb.dma("sp", R2[sub].v, s_x1[t, sub])
                    for g in range(4):
                        pss = [bank(8) for _ in range(4)]
                        for pc in range(4):
                            w_ = rg.get(22 + g * 4 + pc)[:, 0:11 * 512].r("p (j c) -> p j c", j=11)
                            for sub in range(4):
                                for jj in range(11):
                                    j = pc * 11 + jj
                                    kb.mm(pss[sub].v, A[:, j, sub * 128:(sub + 1) * 128], w_[:, jj, :], j == 0, j == JF - 1,
                                          signal=(jj == 10))
                        for sub in range(4):
                            kb.stt(R2[sub][:, g * 512:(g + 1) * 512], R2[sub][:, g * 512:(g + 1) * 512], ALPHA,
                                   pss[sub].v, ALU.mult, ALU.add)
                    for sub in range(4):
                        r2 = R2[sub].v
                        ln_stats(r2, stat, epst)
                        kb.act(r2, r2, AF.Identity, scale=stat[:, 0:1], bias=stat[:, 1:2])
                        kb.tt("dve", r2, r2, g2rep.v, ALU.mult)
                        kb.tt("pool", r2, r2, b2rep.v, ALU.add)
                        kb.dma("pool", out_d[t * T + sub * 128:t * T + (sub + 1) * 128, :], r2)
                kb.barrier()
        print("kernel: %d instructions emitted" % kb.ninst)
    _DBG["names"] = kb.names
    return nc


def make_in_maps(inp, NT):
    x = np.asarray(inp["x"], np.float32)
    B, N, _ = x.shape
    seg = NT * T
    nseg = N // seg
    ctx = np.asarray(inp["ctx"], np.float32)
    c = np.asarray(inp["c"], np.float32)
    c_ctx = np.asarray(inp["c_ctx"], np.float32)
    f32 = lambda a: np.ascontiguousarray(np.asarray(a, np.float32))
    lbl = f32(np.asarray(inp["lb_logits"]).reshape(2, 2, NH, 128).transpose(3, 0, 1, 2))
    convw = f32(np.asarray(inp["conv_w"])[0].reshape(3, NH, 128).transpose(2, 0, 1))
    bmod = f32(np.asarray(inp["b_mod"])[0])
    shared = {
        "w_mod": f32(inp["w_mod"][0]), "bmod_fm": f32(bmod.reshape(96, 128).T), "b_mod": f32(bmod[None, :]),
        "w_in": f32(inp["w_in"][0]), "lbl": lbl, "g_norm_w": f32(np.asarray(inp["g_norm_w"])[0][None, :]),
        "convw": convw, "w_out": f32(inp["w_out"][0]),
        "ln1_g": f32(np.asarray(inp["ln1_g"])[0][None, :]), "ln1_b": f32(np.asarray(inp["ln1_b"])[0][None, :]),
        "w_gate": f32(inp["w_gate"][0]), "w_up": f32(inp["w_up"][0]), "w_down": f32(inp["w_down"][0]),
        "ln2_g": f32(np.asarray(inp["ln2_g"])[0][None, :]), "ln2_b": f32(np.asarray(inp["ln2_b"])[0][None, :]),
    }
    maps = []
    for r in range(B * nseg):
        b, j = divmod(r, nseg)
        t0 = j * seg
        m = dict(shared)
        m["x"] = f32(x[b, t0:t0 + seg])
        xhf = np.zeros((T, D), np.float32)
        xhb = np.zeros((T, D), np.float32)
        vmf = np.ones(T, np.float32)
        vmb = np.ones(T, np.float32)
        sel = np.zeros((128, 2), np.float32)
        if j == 0:
            xhf[T - CTX:] = ctx[b]
            vmf[:T - CTX] = 0.0
            sel[:, 0] = 1.0
        else:
            xhf[:] = x[b, t0 - T:t0]
        if j == nseg - 1:
            xhb[:CTX] = ctx[b]
            vmb[CTX:] = 0.0
            sel[:, 1] = 1.0
        else:
            xhb[:] = x[b, t0 + seg:t0 + seg + T]
        m["xhf"], m["xhb"], m["sel"] = xhf, xhb, sel
        m["vm"] = f32(np.concatenate([vmf.reshape(4, 128).T, vmb.reshape(4, 128).T], axis=1))
        cv = np.stack([c[b].reshape(KC, 128).T, c_ctx.reshape(KC, 128).T], axis=2)
        m["cvec"] = f32(cv)
        maps.append(m)
    return maps, B, nseg, seg


_NC_CACHE = {}


def kernel(**inputs):
    x = np.asarray(inputs["x"])
    B, N, _ = x.shape
    NT = N // (4 * T)
    maps, B, nseg, seg = make_in_maps(inputs, NT)
    if NT not in _NC_CACHE:
        _NC_CACHE[NT] = build_program(NT)
    nc = _NC_CACHE[NT]
    res = run_bass_kernel_spmd(nc, maps, core_ids=list(range(len(maps))))
    out = np.empty((B, N, D), np.float32)
    for r in range(B * nseg):
        b, j = divmod(r, nseg)
        out[b, j * seg:(j + 1) * seg] = res.results[r]["out"]
    return out
```

```python
import numpy as np
import concourse.bass as bass
import concourse.mybir as mybir
from concourse.bass_utils import run_bass_kernel_spmd
from contextlib import ExitStack

F32 = mybir.dt.float32
BF16 = mybir.dt.bfloat16
AF = mybir.ActivationFunctionType
ALU = mybir.AluOpType

D = 2048
KC = 16
NH = 8
DFF = 5632
JF = 44
T = 512
CTX = 256
ALPHA = 2.0 ** 0.25
EPS = 1e-6


class View:
    __slots__ = ("b", "ap")

    def __init__(self, b, ap):
        self.b = b
        self.ap = ap

    def __getitem__(self, idx):
        return View(self.b, self.ap[idx])

    def r(self, pat, **kw):
        return View(self.b, self.ap.rearrange(pat, **kw))

    def bc(self, shape):
        return View(self.b, self.ap.broadcast_to(list(shape)))

    def un(self, d):
        return View(self.b, self.ap.unsqueeze(d))


class Buf:
    __slots__ = ("t", "w", "r", "name", "dcnt", "kind")

    def __init__(self, t, name, kind):
        self.t = t
        self.name = name
        self.kind = kind
        self.w = {}
        self.r = {}
        self.dcnt = 0

    def __getitem__(self, idx):
        return View(self, self.t[idx])

    @property
    def v(self):
        return View(self, self.t[:])


class KB:
    def __init__(self, nc, es):
        self.nc = nc
        self.es = es
        self.eng = {"pe": nc.tensor, "act": nc.scalar, "dve": nc.vector,
                    "pool": nc.gpsimd, "sp": nc.sync}
        self.sems = {}
        self.cnt = {}
        self.waited = {}
        self.keycnt = {}
        for e in self.eng:
            self.sems[e] = es.enter_context(nc.semaphore("s_" + e))
            self.cnt[e] = 0
            self.waited[e] = {}
        self.ninst = 0
        self.uid = 0
        self.names = {}

    def sbuf(self, es, name, shape, dt):
        self.uid += 1
        t = es.enter_context(self.nc.sbuf_tensor("%s_%d" % (name, self.uid), list(shape), dt))
        self.names[name] = "%s_%d" % (name, self.uid)
        return Buf(t, name, "sb")

    def psum(self, es, name, shape, dt):
        t = es.enter_context(self.nc.psum_tensor(name, list(shape), dt))
        return Buf(t, name, "ps")

    def dram(self, name, shape, dt, kind="Internal"):
        t = self.nc.dram_tensor(name, list(shape), dt, kind=kind)
        return Buf(t.ap(), name, "dr")

    def _key(self, b, q="sp"):
        k = "d_" + b.name + ("_sw" if q == "pool" else "")
        if k not in self.sems:
            self.sems[k] = self.es.enter_context(self.nc.semaphore(k))
            self.keycnt[k] = 0
        return k

    def _waits(self, e, reads, writes):
        need = {}
        for b in reads:
            for k, v in b.w.items():
                if need.get(k, 0) < v:
                    need[k] = v
        for b in writes:
            for k, v in b.w.items():
                if need.get(k, 0) < v:
                    need[k] = v
            for k, v in b.r.items():
                if need.get(k, 0) < v:
                    need[k] = v
        wd = self.waited[e]
        for k, v in need.items():
            if e == "pe" and k == "pe":
                continue
            if wd.get(k, 0) < v:
                self.eng[e].wait_ge(self.sems[k], v)
                wd[k] = v

    def op(self, e, fn, reads=(), writes=(), quiet=False):
        if _DBG.get("halt"):
            return None
        ps_r = [b for b in reads if b.kind == "ps"]
        if ps_r:
            writes = list(writes) + ps_r
        self._waits(e, reads, writes)
        inst = fn(self.eng[e])
        self.ninst += 1
        if quiet:
            tok = self.cnt[e] + 1
        else:
            self.cnt[e] += 1
            tok = self.cnt[e]
            inst.then_inc(self.sems[e], 1)
        for b in reads:
            if b.r.get(e, 0) < tok:
                b.r[e] = tok
        for b in writes:
            if b.w.get(e, 0) < tok:
                b.w[e] = tok
        return inst

    def dma(self, q, out, in_, key=None):
        if _DBG.get("halt"):
            return
        src, dst = in_.b, out.b
        if key is None:
            key = dst if dst.kind == "sb" else (src if src.kind == "sb" else dst)
        self._waits(q, [src], [dst] if dst.kind != "dr" else [])
        k = self._key(key, q)
        inst = self.eng[q].dma_start(out=out.ap, in_=in_.ap)
        self.keycnt[k] += 16
        inst.then_inc(self.sems[k], 16)
        self.ninst += 1
        v = self.keycnt[k]
        if src.kind != "dr":
            src.r[k] = max(src.r.get(k, 0), v)
        dst.w[k] = max(dst.w.get(k, 0), v)

    def barrier(self):
        for e in self.eng:
            wd = self.waited[e]
            for k in list(self.sems.keys()):
                v = self.cnt[k] if k in self.cnt else self.keycnt[k]
                if k == e or v == 0:
                    continue
                if wd.get(k, 0) < v:
                    self.eng[e].wait_ge(self.sems[k], v)
                    wd[k] = v

    @staticmethod
    def _bufs(*vs):
        return [v.b for v in vs if isinstance(v, View)]

    @staticmethod
    def _a(v):
        return v.ap if isinstance(v, View) else v

    def mm(self, out, lhsT, rhs, start, stop, signal=False):
        self.op("pe", lambda e: e.matmul(out.ap, lhsT=lhsT.ap, rhs=rhs.ap, start=start, stop=stop),
                reads=[lhsT.b, rhs.b], writes=[out.b], quiet=not (stop or signal))

    def tr(self, out, in_, ident):
        self.op("pe", lambda e: e.transpose(out=out.ap, in_=in_.ap, identity=ident.ap),
                reads=[in_.b, ident.b], writes=[out.b])

    def act(self, out, in_, func, scale=1.0, bias=0.0, accum=None):
        if isinstance(scale, View) and not isinstance(bias, View):
            bias = self.cst[:, 0:1] if bias == 0.0 else self.cst[:, 1:2]
            if func == AF.Copy:
                func = AF.Identity
        rd = self._bufs(in_, scale, bias)
        wr = [out.b] + ([accum.b] if accum is not None else [])
        kw = {}
        if accum is not None:
            kw["accum_out"] = accum.ap
        self.op("act", lambda e: e.activation(out=out.ap, in_=in_.ap, func=func, scale=self._a(scale),
                                              bias=self._a(bias), **kw), reads=rd, writes=wr)

    def tt(self, eng, out, in0, in1, op):
        self.op(eng, lambda e: e.tensor_tensor(out=out.ap, in0=in0.ap, in1=in1.ap, op=op),
                reads=[in0.b, in1.b], writes=[out.b])

    def ts(self, eng, out, in0, s1, op0, s2=None, op1=None):
        rd = self._bufs(in0, s1, s2)
        if op1 is None:
            self.op(eng, lambda e: e.tensor_scalar(out=out.ap, in0=in0.ap, scalar1=self._a(s1), scalar2=None,
                                                   op0=op0), reads=rd, writes=[out.b])
        else:
            self.op(eng, lambda e: e.tensor_scalar(out=out.ap, in0=in0.ap, scalar1=self._a(s1),
                                                   scalar2=self._a(s2), op0=op0, op1=op1), reads=rd, writes=[out.b])

    def stt(self, out, in0, scalar, in1, op0, op1):
        rd = self._bufs(in0, scalar, in1)
        self.op("dve", lambda e: e.scalar_tensor_tensor(out=out.ap, in0=in0.ap, scalar=self._a(scalar),
                                                        in1=in1.ap, op0=op0, op1=op1), reads=rd, writes=[out.b])

    def copy(self, eng, out, in_):
        if eng == "act":
            self.act(out, in_, AF.Copy)
        else:
            self.op(eng, lambda e: e.tensor_copy(out=out.ap, in_=in_.ap), reads=[in_.b], writes=[out.b])

    def memset(self, eng, out, val):
        self.op(eng, lambda e: e.memset(out.ap, val), writes=[out.b])


_DBG = {}


class StopBuild(Exception):
    pass


def chk(n):
    if _DBG.get("stop") == n:
        _DBG["halt"] = True

def build_program(NT):
    _DBG["halt"] = False
    nc = bass.Bass("TRN2", target_bir_lowering=False)
    NTOK = NT * T
    ges = ExitStack()
    with ges:
        kb = KB(nc, ges)

        def ext_in(name, shape, dt=F32):
            if _DBG.get("noweights") and name in ("w_mod", "w_in", "w_out", "w_gate", "w_up", "w_down"):
                return kb.dram(name, shape, dt)
            return Buf(nc.dram_tensor(name, list(shape), dt, kind="ExternalInput").ap(), name, "dr")

        x_d = ext_in("x", [NTOK, D])
        xhf_d = ext_in("xhf", [T, D])
        xhb_d = ext_in("xhb", [T, D])
        vm_d = ext_in("vm", [128, 8])
        sel_d = ext_in("sel", [128, 2])
        cvec_d = ext_in("cvec", [128, KC, 2])
        wmod_d = ext_in("w_mod", [D, 6 * D])
        bmodfm_d = ext_in("bmod_fm", [128, 96])
        bmod_d = ext_in("b_mod", [1, 6 * D])
        win_d = ext_in("w_in", [D, 8192])
        lbl_d = ext_in("lbl", [128, 2, 2, NH])
        gw_d = ext_in("g_norm_w", [1, 128])
        cw_d = ext_in("convw", [128, 3, NH])
        wout_d = ext_in("w_out", [D, D])
        ln1g_d = ext_in("ln1_g", [1, D])
        ln1b_d = ext_in("ln1_b", [1, D])
        wg_d = ext_in("w_gate", [D, DFF])
        wu_d = ext_in("w_up", [D, DFF])
        wd_d = ext_in("w_down", [DFF, D])
        ln2g_d = ext_in("ln2_g", [1, D])
        ln2b_d = ext_in("ln2_b", [1, D])
        out_d = Buf(nc.dram_tensor("out", [NTOK, D], F32, kind="ExternalOutput").ap(), "out", "dr")

        wfm = kb.dram("wfm", [16, 128, KC, 384], BF16)
        wtm = kb.dram("wtm", [NH, 128, KC, 256], BF16)
        wo = kb.dram("wo", [4, 128, KC, 512], BF16)
        wgu = kb.dram("wgu", [22, 128, KC, 512], BF16)
        wdn = kb.dram("wdn", [4, 128, JF, 512], BF16)
        s_v = kb.dram("s_v", [NT, NH, 128, 512], BF16)
        s_sg = kb.dram("s_sg", [NT, NH, 128, 512], F32)
        s_of = kb.dram("s_of", [NT, NH, 128, 512], F32)
        s_qb = kb.dram("s_qb", [NT, NH, 128, 768], BF16)
        s_kb = kb.dram("s_kb", [NT, NH, 128, 512], BF16)
        s_eb = kb.dram("s_eb", [NT, NH, 128, 8], F32)
        s_yc = kb.dram("s_yc", [NT, NH, 128, 512], BF16)
        s_x1 = kb.dram("s_x1", [NT, 4, 128, D], F32)
        s_h2 = kb.dram("s_h2", [NT, 128, KC, 512], BF16)

        pb = [kb.psum(ges, "pb%d" % i, [128, 512], F32) for i in range(8)]
        rot = {"i": 0}

        def bank(n=4, base=0):
            i = rot["i"] % n
            rot["i"] = (i + 1) % n
            return pb[base + i]

        def bfv(b):
            return View(b, b.t[:].bitcast(BF16))

        ident = kb.sbuf(ges, "ident", [128, 128], BF16)
        identf = kb.sbuf(ges, "identf", [128, 128], F32)
        maskF = kb.sbuf(ges, "maskF", [128, 128], F32)
        maskB = kb.sbuf(ges, "maskB", [128, 128], F32)
        rmask = kb.sbuf(ges, "rmask", [128, T], F32)
        modv = kb.sbuf(ges, "modv", [128, 12, KC], F32)
        oml = kb.sbuf(ges, "oml", [128, 2, 2, NH], F32)
        gwrep = kb.sbuf(ges, "gwrep", [128, 128], F32)
        convw = kb.sbuf(ges, "convw", [128, 3, NH], F32)
        vm = kb.sbuf(ges, "vm", [128, 8], F32)
        sel = kb.sbuf(ges, "sel", [128, 2], F32)
        sstart = kb.sbuf(ges, "sstart", [128, 2, NH, 128], F32)
        ring = [kb.sbuf(ges, "ring%d" % i, [128, 8192], BF16) for i in range(3)]
        M_SCLA, M_SHFA, M_SCLF, M_SHFF, M_SCLC, M_SHFC, M_SCLHF, M_SHFHF, M_SCLHB, M_SHFHB = range(10)

        cst = kb.sbuf(ges, "cst", [128, 2], F32)
        kb.memset("dve", cst[:, 0:1], 0.0)
        kb.memset("dve", cst[:, 1:2], 1.0)
        kb.cst = cst
        kb.memset("pool", identf.v, 1.0)
        kb.op("pool", lambda e: e.affine_select(out=identf.t[:], in_=identf.t[:], pattern=[[-1, 128]],
                                                compare_op=ALU.is_equal, fill=0.0, base=0, channel_multiplier=1),
              reads=[identf], writes=[identf])
        kb.copy("dve", ident.v, identf.v)
        kb.memset("pool", maskF.v, 1.0)
        kb.op("pool", lambda e: e.affine_select(out=maskF.t[:], in_=maskF.t[:], pattern=[[1, 128]],
                                                compare_op=ALU.is_ge, fill=0.0, base=0, channel_multiplier=-1),
              reads=[maskF], writes=[maskF])
        kb.memset("pool", maskF[0:64, 64:128], 0.0)
        kb.memset("pool", maskB.v, 1.0)
        kb.op("pool", lambda e: e.affine_select(out=maskB.t[:], in_=maskB.t[:], pattern=[[-1, 128]],
                                                compare_op=ALU.is_ge, fill=0.0, base=0, channel_multiplier=1),
              reads=[maskB], writes=[maskB])
        kb.memset("pool", maskB[64:128, 0:64], 0.0)
        kb.memset("dve", rmask.v, 1.0)
        kb.memset("dve", rmask.v.r("p (c k) -> p c k", k=64)[:, :, 0:1], 0.0)
        kb.dma("sp", gwrep.v, gw_d[0:1, :].bc([128, 128]))
        kb.dma("sp", convw.v, cw_d.v)
        kb.dma("sp", vm.v, vm_d.v)
        kb.dma("sp", sel.v, sel_d.v)

        with ExitStack() as ses0:
            garep = kb.sbuf(ses0, "garep", [128, 2, D], F32)
            with ExitStack() as ses:
                cvec = kb.sbuf(ses, "cvec", [128, KC, 2], F32)
                scv = kb.sbuf(ses, "scv", [128, KC, 2], F32)
                screp = kb.sbuf(ses, "screp", [128, KC, 128], F32)
                bmodfm = kb.sbuf(ses, "bmodfm", [128, 96], F32)
                modfm = kb.sbuf(ses, "modfm", [128, 96, 2], F32)
                lbl = kb.sbuf(ses, "lbl", [128, 2, 2, NH], F32)
                wmt = [kb.sbuf(ses, "wmt%d" % i, [128, KC, 512], F32) for i in range(2)]

                kb.dma("sp", cvec.v, cvec_d.v)
                kb.dma("sp", garep[:, 0, :], bmod_d[0:1, 2 * D:3 * D].bc([128, D]))
                kb.dma("sp", garep[:, 1, :], bmod_d[0:1, 5 * D:6 * D].bc([128, D]))
                kb.dma("sp", bmodfm.v, bmodfm_d.v)
                kb.dma("sp", lbl.v, lbl_d.v)
                kb.act(scv.v, cvec.v, AF.Silu)
                kb.copy("dve", screp.v, scv[:, :, 0:1].bc([128, KC, 128]))
                kb.tt("dve", lbl[:, :, 0, :], lbl[:, :, 1, :], lbl[:, :, 0, :], ALU.subtract)
                kb.act(oml[:, :, 0, :], lbl[:, :, 0, :], AF.Sigmoid)
                kb.ts("dve", oml[:, :, 1, :], oml[:, :, 0, :], -1.0, ALU.mult)

                wm_v = wmod_d.v.r("(kc p) n -> p kc n", p=128)
                for blk in range(24):
                    fam = blk // 4
                    wt_ = wmt[blk % 2]
                    kb.dma("sp", wt_.v, wm_v[:, :, blk * 512:(blk + 1) * 512])
                    if fam in (2, 5):
                        g = 0 if fam == 2 else 1
                        col = (blk % 4) * 512
                        ps = bank(8)
                        for kc in range(KC):
                            kb.mm(ps.v, screp[:, kc, :], wt_[:, kc, :], kc == 0, kc == KC - 1)
                        kb.tt("dve", garep[:, g, col:col + 512], ps.v, garep[:, g, col:col + 512], ALU.add)
                    else:
                        ps = bank(8)
                        for cc in range(4):
                            for kc in range(KC):
                                kb.mm(ps[:, cc * 2:cc * 2 + 2], wt_[:, kc, cc * 128:(cc + 1) * 128], scv[:, kc, :],
                                      kc == 0, kc == KC - 1)
                        kb.tt("dve", modfm[:, blk * 4:(blk + 1) * 4, :],
                              ps[:, 0:8].r("p (c t) -> p c t", t=2),
                              bmodfm[:, blk * 4:(blk + 1) * 4].un(2).bc([128, 4, 2]), ALU.add)

                def mfam(f, which):
                    return modfm[:, f * KC:(f + 1) * KC, which]

                kb.ts("dve", modv[:, M_SCLA, :], mfam(1, 0), 1.0, ALU.add)
                kb.copy("dve", modv[:, M_SHFA, :], mfam(0, 0))
                kb.ts("dve", modv[:, M_SCLF, :], mfam(4, 0), 1.0, ALU.add)
                kb.copy("dve", modv[:, M_SHFF, :], mfam(3, 0))
                kb.ts("dve", modv[:, M_SCLC, :], mfam(1, 1), 1.0, ALU.add)
                kb.copy("dve", modv[:, M_SHFC, :], mfam(0, 1))
                for (dst_s, dst_h, sc) in ((M_SCLHF, M_SHFHF, 0), (M_SCLHB, M_SHFHB, 1)):
                    for (dst, own, cx) in ((dst_s, M_SCLA, M_SCLC), (dst_h, M_SHFA, M_SHFC)):
                        kb.tt("dve", modv[:, dst, :], modv[:, cx, :], modv[:, own, :], ALU.subtract)
                        kb.stt(modv[:, dst, :], modv[:, dst, :], sel[:, sc:sc + 1], modv[:, own, :],
                               ALU.mult, ALU.add)
                kb.barrier()

            with ExitStack() as ses:
              if _DBG.get("stop") != 5:
                    csrc = [kb.sbuf(ses, "csrc%d" % i, [128, 8192], F32) for i in range(2)]
                    cdst = [kb.sbuf(ses, "cdst%d" % i, [128, 8192], BF16) for i in range(2)]
                    engs = ["act", "dve", "pool"]
                    blocks = []

                    for kc in range(KC):
                        def loads(s_, kc=kc):
                            return [(s_.v, win_d[kc * 128:(kc + 1) * 128, :])]

                        def casts(s_, d_, i):
                            sv = s_.v.r("p (f h c) -> p f h c", f=8, h=NH)
                            fm = d_[:, 0:6144].r("p (u g c) -> p u g c", u=16, g=3)
                            tmv = d_[:, 6144:8192].r("p (h g c) -> p h g c", h=NH, g=2)
                            plan = [(fm[:, 0:8, 0, :], 0), (fm[:, 0:8, 1, :], 1), (fm[:, 0:8, 2, :], 3),
                                    (fm[:, 8:16, 0, :], 5), (fm[:, 8:16, 1, :], 6), (fm[:, 8:16, 2, :], 7),
                                    (tmv[:, :, 0, :], 2), (tmv[:, :, 1, :], 4)]
                            for n, (dv, f) in enumerate(plan):
                                kb.copy(engs[(n + i) % 3], dv, sv[:, f, :, :])

                        def stores(d_, kc=kc):
                            return [(wfm[:, :, kc, :].r("u p c -> p u c"), d_[:, 0:6144].r("p (u c) -> p u c", u=16)),
                                    (wtm[:, :, kc, :].r("h p c -> p h c"), d_[:, 6144:8192].r("p (h c) -> p h c", h=NH))]
                        blocks.append((loads, casts, stores))
                    for kc in range(KC):
                        def loads(s_, kc=kc):
                            return [(s_[:, 0:2048], wout_d[kc * 128:(kc + 1) * 128, :])]

                        def casts(s_, d_, i):
                            for q in range(4):
                                kb.tt(["dve", "pool"][q % 2], d_[:, q * 512:(q + 1) * 512], s_[:, q * 512:(q + 1) * 512],
                                      garep[:, 0, q * 512:(q + 1) * 512], ALU.mult)

                        def stores(d_, kc=kc):
                            return [(wo[:, :, kc, :].r("g p c -> p g c"), d_[:, 0:2048].r("p (g c) -> p g c", g=4))]
                        blocks.append((loads, casts, stores))
                    for kc in range(KC):
                        for hf in range(2):
                            def loads(s_, kc=kc, hf=hf):
                                c0 = hf * 2816
                                return [(s_[:, 0:2816], wg_d[kc * 128:(kc + 1) * 128, c0:c0 + 2816]),
                                        (s_[:, 2816:5632], wu_d[kc * 128:(kc + 1) * 128, c0:c0 + 2816])]

                            def casts(s_, d_, i):
                                dv = d_[:, 0:5632].r("p (j gu c) -> p j gu c", j=22, gu=2)
                                kb.copy(engs[i % 3], dv[:, :, 0, :], s_[:, 0:2816].r("p (j c) -> p j c", j=22))
                                kb.copy(engs[(i + 1) % 3], dv[:, :, 1, :], s_[:, 2816:5632].r("p (j c) -> p j c", j=22))

                            def stores(d_, kc=kc, hf=hf):
                                return [(wgu[hf * 11:(hf + 1) * 11, :, kc, :].r("jg p c -> p jg c"),
                                         d_[:, 0:5632].r("p (jg c) -> p jg c", jg=11))]
                            blocks.append((loads, casts, stores))
                    for jb in range(JF // 4):
                        def loads(s_, jb=jb):
                            return [(s_.v.r("p (j n) -> p j n", j=4),
                                     wd_d[jb * 512:(jb + 1) * 512, :].r("(j p) n -> p j n", p=128))]

                        def casts(s_, d_, i):
                            for q in range(4):
                                kb.tt(["dve", "pool"][q % 2], d_[:, q * 2048:(q + 1) * 2048],
                                      s_[:, q * 2048:(q + 1) * 2048], garep[:, 1, :], ALU.mult)

                        def stores(d_, jb=jb):
                            dv4 = d_.v.r("p (j g c) -> p j g c", j=4, g=4)
                            return [(wdn[g, :, jb * 4:(jb + 1) * 4, :], dv4[:, :, g, :]) for g in range(4)]
                        blocks.append((loads, casts, stores))

                    def do_loads(i):
                        for (dv, sv) in blocks[i][0](csrc[i % 2]):
                            kb.dma("sp", dv, sv)
                    do_loads(0)
                    for i in range(len(blocks)):
                        if i + 1 < len(blocks):
                            do_loads(i + 1)
                        blocks[i][1](csrc[i % 2], cdst[i % 2], i)
                        for (dv, sv) in blocks[i][2](cdst[i % 2]):
                            kb.dma("pool", dv, sv)
                    kb.barrier()

        def ln_stats(xin, stat, eps_t):
            for q in range(4):
                kb.op("dve", lambda e, q=q: e.bn_stats(out=stat.t[:, 8 + q * 6:14 + q * 6],
                                                       in_=xin.ap[:, q * 512:(q + 1) * 512]),
                      reads=[xin.b], writes=[stat])
            kb.op("dve", lambda e: e.bn_aggr(out=stat.t[:, 2:4], in_=stat.t[:, 8:32]), reads=[stat], writes=[stat])
            kb.act(stat[:, 4:5], stat[:, 3:4], AF.Sqrt, bias=eps_t[:, 0:1])
            kb.op("dve", lambda e: e.reciprocal(out=stat.t[:, 0:1], in_=stat.t[:, 4:5]), reads=[stat], writes=[stat])
            kb.stt(stat[:, 1:2], stat[:, 2:3], -1.0, stat[:, 0:1], ALU.mult, ALU.mult)

        def to_fm(xn, hdst, col0, scl_i, shf_i):
            for half in range(2):
                ps = bank(4)
                pv = bfv(ps)
                for k8 in range(8):
                    kc = half * 8 + k8
                    kb.tr(pv[:, k8 * 128:(k8 + 1) * 128], xn[:, kc * 128:(kc + 1) * 128], ident.v)
                for k8 in range(8):
                    kc = half * 8 + k8
                    if k8 % 2 == 0:
                        kb.ts("dve", hdst[:, kc, col0:col0 + 128], pv[:, k8 * 128:(k8 + 1) * 128],
                              modv[:, scl_i, kc:kc + 1], ALU.mult, modv[:, shf_i, kc:kc + 1], ALU.add)
                    else:
                        kb.act(hdst[:, kc, col0:col0 + 128], pv[:, k8 * 128:(k8 + 1) * 128], AF.Identity,
                               scale=modv[:, scl_i, kc:kc + 1], bias=modv[:, shf_i, kc:kc + 1])

        class Ring:
            def __init__(self, items, hold=1):
                self.hold = hold
                self.items = items
                self.issued = 0

            def get(self, i):
                while self.issued < min(len(self.items), i + 4 - self.hold):
                    n = self.issued
                    dv, ncols, pat = self.items[n]
                    slot = ring[n % 3]
                    sv = slot[:, 0:ncols]
                    if pat is not None:
                        sv = sv.r(pat[0], **pat[1])
                    kb.dma("sp", sv, dv)
                    self.issued += 1
                return ring[i % 3]

        scan_rot = {"i": 0}

        def scan_dir(es_s, fwd, Qp, Kt, Vh, Eend, S32, SB, tmpb, O_out, add_to, bset=None):
            KT_A, KT_B, PT, sbf, tmp = tmpb["KT_A"], tmpb["KT_B"], tmpb["PT"], tmpb["sbf"], tmpb["tmp"]
            mask = maskF if fwd else maskB
            pairs = list(range(4)) if fwd else list(range(3, -1, -1))
            base = 4 if (bset is None or bset == 0) else 0
            B_sc, B_tr, B_u0, B_u1 = pb[base], pb[base + 1], pb[base + 2], pb[base + 3]
            trv = bfv(B_tr)
            for pi, p in enumerate(pairs):
                ksl = Kt[:, p * 128:(p + 1) * 128]
                if Qp is not None:
                    qpair = Qp[:, p * 192:(p + 1) * 192].r("p (a c) -> p a c", c=64)[:, 0:3:2, :]
                    kb.mm(B_sc[:, pi * 128:(pi + 1) * 128], ksl, qpair, True, True)
                kb.tr(trv[:, pi * 128:(pi + 1) * 128], ksl, ident.v)
            for pi, p in enumerate(pairs):
                if Qp is not None:
                    kb.tt("dve", PT[pi].v, B_sc[:, pi * 128:(pi + 1) * 128], mask.v, ALU.mult)
                kb.copy("act", KT_A[pi][0:64, :], trv[0:64, pi * 128:(pi + 1) * 128])
                kb.copy("act", KT_B[pi][64:128, :], trv[64:128, pi * 128:(pi + 1) * 128])
            yield
            seq = []
            for pi, p in enumerate(pairs):
                order = ((KT_A, 2 * p), (KT_B, 2 * p + 1)) if fwd else ((KT_B, 2 * p + 1), (KT_A, 2 * p))
                for ci_, (KTx, ch) in enumerate(order):
                    j = 2 * pi + ci_
                    ups = (B_u0 if j < 4 else B_u1)[:, (j % 4) * 128:(j % 4 + 1) * 128]
                    kb.mm(ups, KTx[pi].v, Vh[:, p, :], True, True)
                    seq.append((ups, ch))
                yield
            states = [SB]
            for j, (ups, ch) in enumerate(seq):
                tj = tmp[j % 2]
                kb.tt("dve", tj.v, ups, S32, ALU.add)
                kb.ts("dve", S32, tj.v, Eend[:, ch:ch + 1], ALU.mult)
                kb.act(sbf[j].v, tj.v, AF.Copy, scale=Eend[:, ch:ch + 1])
                states.append(sbf[j].v)
                if j % 2 == 1:
                    yield
            if Qp is not None:
                for pi, p in enumerate(pairs):
                    ops_ = B_sc[:, pi * 128:(pi + 1) * 128]
                    q_lo = Qp[:, p * 192:p * 192 + 128]
                    q_hi = Qp[:, p * 192 + 64:p * 192 + 192]
                    q1, q2 = (q_lo, q_hi) if fwd else (q_hi, q_lo)
                    kb.mm(ops_, PT[pi].v, Vh[:, p, :], True, False)
                    kb.mm(ops_, q1, states[2 * pi], False, False)
                    kb.mm(ops_, q2, states[2 * pi + 1], False, True)
                    if add_to is None:
                        kb.copy("act", O_out[:, p, :], ops_)
                    else:
                        kb.tt("dve", O_out[:, p, :], ops_, add_to[:, p, :], ALU.add)
                    yield
            kb.copy("act", SB, states[8])

        def drain(g):
            if g is not None:
                for _ in g:
                    pass

        def interleave(main, filler, k=1, first=None):
            n_y = 0
            for _ in main:
                n_y += 1
                if filler is not None:
                    for _i in range(first if (first and n_y == 1) else k):
                        try:
                            next(filler)
                        except StopIteration:
                            filler = None
                            break
            return filler

        def scan_tmps(es_s):
            tb = {"KT_A": [kb.sbuf(es_s, "KT_A%d" % i, [128, 128], BF16) for i in range(4)],
                  "KT_B": [kb.sbuf(es_s, "KT_B%d" % i, [128, 128], BF16) for i in range(4)],
                  "PT": [kb.sbuf(es_s, "PT%d" % i, [128, 128], BF16) for i in range(4)],
                  "tmp": [kb.sbuf(es_s, "stmp%d" % i, [128, 128], F32) for i in range(2)],
                  "sbf": [kb.sbuf(es_s, "sbf%d" % i, [128, 128], BF16) for i in range(8)]}
            for i in range(4):
                kb.memset("pool", tb["KT_A"][i].v, 0.0)
                kb.memset("pool", tb["KT_B"][i].v, 0.0)
            return tb

        try:
         with ExitStack() as s1:
           if _DBG.get("stop") not in (4, 5):
                 xs = [kb.sbuf(s1, "xs%d" % i, [128, D], F32) for i in range(2)]
                 xn = [kb.sbuf(s1, "xn%d" % i, [128, D], BF16) for i in range(2)]
                 stat = kb.sbuf(s1, "stat", [128, 32], F32)
                 epst = kb.sbuf(s1, "epst", [128, 1], F32)
                 hT = kb.sbuf(s1, "hT", [128, KC, T], BF16)
                 Vh = [kb.sbuf(s1, "Vh%d" % i, [128, 4, 128], BF16) for i in range(3)]
                 SGh = [kb.sbuf(s1, "SGh%d" % i, [128, 4, 128], F32) for i in range(3)]
                 Ebf = [kb.sbuf(s1, "Ebf%d" % i, [128, 8], F32) for i in range(2)]
                 Oh = [kb.sbuf(s1, "Oh%d" % i, [128, 4, 128], F32) for i in range(2)]
                 qs = kb.sbuf(s1, "qs", [128, T], F32)
                 sg_ = [kb.sbuf(s1, "sg%d" % i, [128, T], F32) for i in range(2)]
                 kk_ = [kb.sbuf(s1, "kk%d" % i, [128, T], F32) for i in range(2)]
                 lf_ = [kb.sbuf(s1, "lf%d" % i, [128, T], F32) for i in range(2)]
                 bb_ = [kb.sbuf(s1, "bb%d" % i, [128, T], F32) for i in range(2)]
                 e1_ = [kb.sbuf(s1, "e1%d" % i, [128, T], F32) for i in range(2)]
                 e2_ = [kb.sbuf(s1, "e2%d" % i, [128, T], F32) for i in range(2)]
                 Qp_ = [kb.sbuf(s1, "Qp%d" % i, [128, 768], BF16) for i in range(4)]
                 Kt_ = [kb.sbuf(s1, "Kt%d" % i, [128, T], BF16) for i in range(4)]
                 Eb_ = [kb.sbuf(s1, "Eb%d" % i, [128, 8], F32) for i in range(2)]
                 S32f = kb.sbuf(s1, "S32f", [128, NH, 128], F32)
                 SBf = kb.sbuf(s1, "SBf", [128, NH, 128], BF16)
                 S32h = kb.sbuf(s1, "S32h", [128, NH, 128], F32)
                 SBh = kb.sbuf(s1, "SBh", [128, NH, 128], BF16)
                 xv_s = kb.sbuf(s1, "xv_s", [128, T], F32)
                 zc = kb.sbuf(s1, "zc", [128, T], F32)
                 acc = kb.sbuf(s1, "acc", [128, T], F32)
                 yc = [kb.sbuf(s1, "yc%d" % i, [128, T], BF16) for i in range(2)]
                 tb = scan_tmps(s1)
                 kb.memset("dve", epst.v, EPS)
                 for q_ in Qp_:
                     kb.memset("pool", q_.v, 0.0)
                 chk(10)

                 def ln_tile(src_rows, scl_i, shf_i):
                     for sub in range(4):
                         xb_, xnb = xs[sub % 2], xn[sub % 2]
                         kb.dma("sp", xb_.v, src_rows(sub))
                         ln_stats(xb_.v, stat, epst)
                         kb.act(xnb.v, xb_.v, AF.Identity, scale=stat[:, 0:1], bias=stat[:, 1:2])
                         to_fm(xnb, hT, sub * 128, scl_i, shf_i)

                 def gates(fps, d, h, i2):
                     sg, kk, lf, bb, e1, e2 = sg_[i2], kk_[i2], lf_[i2], bb_[i2], e1_[i2], e2_[i2]
                     kb.act(sg.v, fps, AF.Sigmoid, scale=-1.0)
                     kb.ts("pool", kk.v, sg.v, oml[:, d, 0, h:h + 1], ALU.mult)
                     kb.act(lf.v, sg.v, AF.Ln, scale=oml[:, d, 1, h:h + 1], bias=1.0)
                     kb.op("dve", lambda e: e.tensor_tensor_scan(out=bb.t[:], data0=rmask.t[:], data1=lf.t[:], initial=0.0,
                                                                 op0=ALU.mult, op1=ALU.add),
                           reads=[rmask, lf], writes=[bb])
                     cum = bb
                     if d == 1:
                         kb.tt("pool", lf.v, lf.v, bb.v, ALU.subtract)
                         kb.tt("pool", sg.v.r("p (c k) -> p c k", k=64), lf.v.r("p (c k) -> p c k", k=64),
                               bb.v.r("p (c k) -> p c k", k=64)[:, :, 63:64].bc([128, 8, 64]), ALU.add)
                         cum = sg
                     kb.act(e1.v, cum.v, AF.Exp)
                     kb.act(e2.v, cum.v, AF.Exp, scale=-1.0)
                     return kk, e1, e2

                 def halo(kind):
                     d = 0 if kind == "f" else 1
                     src = xhf_d if d == 0 else xhb_d
                     ln_tile(lambda sub: src[sub * 128:(sub + 1) * 128, :],
                             M_SCLHF if d == 0 else M_SCLHB, M_SHFHF if d == 0 else M_SHFHB)
                     chk(11)
                     items = []
                     for h in range(NH):
                         items.append((wfm[h], KC * 384, ("p (k c) -> p k c", dict(k=KC))))
                         items.append((wtm[h], KC * 256, ("p (k c) -> p k c", dict(k=KC))))
                     rg = Ring(items, hold=2)
                     kb.memset("dve", S32h.v, 0.0)
                     kb.memset("pool", SBh.v, 0.0)
                     hctx = {}

                     def hproj(h):
                         wf_ = rg.get(2 * h)[:, 0:KC * 384].r("p (k c) -> p k c", k=KC)
                         wt_ = rg.get(2 * h + 1)[:, 0:KC * 256].r("p (k c) -> p k c", k=KC)
                         vh = Vh[h % 2]
                         for sub in range(4):
                             ps = bank(4)
                             for kc in range(KC):
                                 kb.mm(ps[:, 0:128], hT[:, kc, sub * 128:(sub + 1) * 128], wt_[:, kc, 0:128],
                                       kc == 0, kc == KC - 1)
                                 if kc % 8 == 7:
                                     yield
                             kb.ts("dve", vh[:, sub, :], ps[:, 0:128], vm[:, d * 4 + sub:d * 4 + sub + 1], ALU.mult)
                         ps = bank(4)
                         for kc in range(KC):
                             kb.mm(ps.v, wf_[:, kc, d * 128:(d + 1) * 128], hT[:, kc, :], kc == 0, kc == KC - 1)
                             if kc % 8 == 7:
                                 yield
                         hctx[h] = ps

                     def hprep(h):
                         kk, e1, e2 = gates(hctx[h].v, d, h, h % 2)
                         kt = Kt_[h % 4]
                         kb.tt("pool", kt.v, kk.v, e2.v, ALU.mult)
                         ee = e1.v.r("p (c k) -> p c k", k=64)
                         eend = ee[:, :, 63] if d == 0 else ee[:, :, 0]
                         kb.copy("dve", Eb_[h % 2].v, eend)

                     drain(hproj(0))
                     hprep(0)
                     for h in range(NH):
                         fil = hproj(h + 1) if h + 1 < NH else None
                         sc = scan_dir(s1, d == 0, None, Kt_[h % 4].v, Vh[h % 2].v, Eb_[h % 2].v, S32h[:, h, :],
                                       SBh[:, h, :], tb, None, None)
                         fil = interleave(sc, fil)
                         drain(fil)
                         if h + 1 < NH:
                             hprep(h + 1)
                     kb.copy("dve", sstart[:, d, :, :], S32h.v)

                 halo("f")
                 chk(12)
                 halo("b")
                 chk(13)
                 kb.copy("dve", S32f.v, sstart[:, 0, :, :])
                 kb.copy("act", SBf.v, sstart[:, 0, :, :])
                 chk(131)

                 for t in range(NT):
                     ln_tile(lambda sub: x_d[t * T + sub * 128:t * T + (sub + 1) * 128, :], M_SCLA, M_SHFA)
                     chk(132)
                     items = []
                     for h in range(NH):
                         items.append((wfm[h], KC * 384, ("p (k c) -> p k c", dict(k=KC))))
                         items.append((wtm[h], KC * 256, ("p (k c) -> p k c", dict(k=KC))))
                     for c in range(NH):
                         items.append((wfm[8 + c], KC * 384, ("p (k c) -> p k c", dict(k=KC))))
                     rg = Ring(items, hold=2)
                     pctx = {}
                     qpad = lambda b_: b_.v.r("p (q a c) -> p q a c", q=4, a=3)[:, :, 0:3:2, :]
                     q4 = lambda v_: v_.r("p (q a c) -> p q a c", q=4, a=2)

                     def proj_head(h):
                         wf_ = rg.get(2 * h)[:, 0:KC * 384].r("p (k c) -> p k c", k=KC)
                         wt_ = rg.get(2 * h + 1)[:, 0:KC * 256].r("p (k c) -> p k c", k=KC)
                         vh, sgh = Vh[h % 3], SGh[h % 3]
                         for sub in range(4):
                             ps = bank(4)
                             for kc in range(KC):
                                 kb.mm(ps[:, 0:256], hT[:, kc, sub * 128:(sub + 1) * 128], wt_[:, kc, :],
                                       kc == 0, kc == KC - 1)
                                 if kc % 8 == 7:
                                     yield
                             kb.copy("dve", vh[:, sub, :], ps[:, 0:128])
                             kb.act(sgh[:, sub, :], ps[:, 128:256], AF.Silu)
                             kb.tt("pool", sgh[:, sub, :], sgh[:, sub, :], gwrep.v, ALU.mult)
                         pf = [bank(4) for _ in range(3)]
                         for g in range(3):
                             for kc in range(KC):
                                 kb.mm(pf[g].v, wf_[:, kc, g * 128:(g + 1) * 128], hT[:, kc, :], kc == 0, kc == KC - 1)
                                 if kc % 8 == 7:
                                     yield
                         pctx[h] = pf

                     def prep(h):
                         pf = pctx[h]
                         kb.act(qs.v, pf[2].v, AF.Silu)
                         kk, e1, e2 = gates(pf[0].v, 0, h, 0)
                         qpf, ktf = Qp_[h % 2], Kt_[h % 2]
                         kb.tt("dve", qpad(qpf), q4(qs.v), q4(e1.v), ALU.mult)
                         kb.tt("pool", ktf.v, kk.v, e2.v, ALU.mult)
                         kb.copy("dve", Ebf[h % 2].v, e1.v.r("p (c k) -> p c k", k=64)[:, :, 63])
                         kk2, e1b, e2b = gates(pf[1].v, 1, h, 1)
                         qpb, ktb = Qp_[2 + h % 2], Kt_[2 + h % 2]
                         kb.tt("dve", qpad(qpb), q4(qs.v), q4(e1b.v), ALU.mult)
                         kb.tt("pool", ktb.v, kk2.v, e2b.v, ALU.mult)
                         kb.copy("dve", Eb_[1].v, e1b.v.r("p (c k) -> p c k", k=64)[:, :, 0])
                         kb.dma("pool", s_qb[t, h], qpb.v)
                         kb.dma("pool", s_kb[t, h], ktb.v)
                         kb.dma("pool", s_eb[t, h], Eb_[1].v)

                     def conv_all():
                         for c in range(NH):
                             wf_ = rg.get(2 * NH + c)[:, 0:KC * 384].r("p (k c) -> p k c", k=KC)
                             pf = [bank(4) for _ in range(3)]
                             for g in range(3):
                                 for kc in range(KC):
                                     kb.mm(pf[g].v, wf_[:, kc, g * 128:(g + 1) * 128], hT[:, kc, :],
                                           kc == 0, kc == KC - 1)
                                     if kc % 8 == 7:
                                         yield
                             kb.copy("act", xv_s.v, pf[2].v)
                             kb.tt("dve", zc.v, pf[1].v, xv_s.v, ALU.mult)
                             z3 = zc.v.r("p (s k) -> p s k", k=64)
                             a3 = acc.v.r("p (s k) -> p s k", k=64)
                             kb.act(acc.v, zc.v, AF.Copy, scale=convw[:, 1, c:c + 1])
                             kb.stt(a3[:, :, 1:64], z3[:, :, 0:63], convw[:, 0, c:c + 1], a3[:, :, 1:64],
                                    ALU.mult, ALU.add)
                             kb.stt(a3[:, :, 0:63], z3[:, :, 1:64], convw[:, 2, c:c + 1], a3[:, :, 0:63],
                                    ALU.mult, ALU.add)
                             kb.tt("dve", yc[c % 2].v, pf[0].v, acc.v, ALU.mult)
                             kb.dma("pool", s_yc[t, c], yc[c % 2].v)

                     drain(proj_head(0))
                     prep(0)
                     drain(proj_head(1))
                     cgen = conv_all()
                     for h in range(NH):
                         vh, sgh, oh = Vh[h % 3], SGh[h % 3], Oh[h % 2]
                         if h + 1 < NH:
                             prep(h + 1)
                         fil = proj_head(h + 2) if h + 2 < NH else cgen
                         sc = scan_dir(s1, True, Qp_[h % 2].v, Kt_[h % 2].v, vh.v, Ebf[h % 2].v, S32f[:, h, :],
                                       SBf[:, h, :], tb, oh.v, None)
                         fil = interleave(sc, fil, first=5)
                         kb.dma("pool", s_v[t, h], vh.v.r("p a c -> p (a c)"))
                         kb.dma("pool", s_sg[t, h], sgh.v.r("p a c -> p (a c)"))
                         kb.dma("pool", s_of[t, h], oh.v.r("p a c -> p (a c)"))
                         if h + 2 < NH:
                             drain(fil)
                     drain(cgen)
                 kb.barrier()
        except StopBuild:
            kb.barrier()

        if _DBG.get("stop", 0) in (0, 2) or _DBG.get("stop", 0) >= 20:
         with ExitStack() as s2:
          if True:
                g1rep = kb.sbuf(s2, "g1rep", [128, D], F32)
                b1rep = kb.sbuf(s2, "b1rep", [128, D], F32)
                stat = kb.sbuf(s2, "stat2", [128, 32], F32)
                epst = kb.sbuf(s2, "epst2", [128, 1], F32)
                rst = kb.sbuf(s2, "rst", [128, 16], F32)
                junk = kb.sbuf(s2, "junk", [128, 128], F32)
                S32b = kb.sbuf(s2, "S32b", [128, NH, 128], F32)
                SBb = kb.sbuf(s2, "SBb", [128, NH, 128], BF16)
                qb = [kb.sbuf(s2, "qb%d" % i, [128, 768], BF16) for i in range(2)]
                kbb = [kb.sbuf(s2, "kbb%d" % i, [128, T], BF16) for i in range(2)]
                ebb = [kb.sbuf(s2, "ebb%d" % i, [128, 8], F32) for i in range(2)]
                vb = [kb.sbuf(s2, "vb%d" % i, [128, 4, 128], BF16) for i in range(2)]
                ofb = [kb.sbuf(s2, "ofb%d" % i, [128, 4, 128], F32) for i in range(2)]
                sgb = [kb.sbuf(s2, "sgb%d" % i, [128, 4, 128], F32) for i in range(2)]
                Yt = kb.sbuf(s2, "Yt", [128, 4, 1024], BF16)
                Yf = kb.sbuf(s2, "Yf", [128, KC, T], BF16)
                R1 = [kb.sbuf(s2, "R1_%d" % i, [128, D], F32) for i in range(4)]
                xn2 = [kb.sbuf(s2, "xn2%d" % i, [128, D], BF16) for i in range(2)]
                H2 = kb.sbuf(s2, "H2", [128, KC, T], BF16)
                tb = scan_tmps(s2)
                tb2 = scan_tmps(s2)
                kb.memset("dve", epst.v, EPS)
                kb.dma("sp", g1rep.v, ln1g_d[0:1, :].bc([128, D]))
                kb.dma("sp", b1rep.v, ln1b_d[0:1, :].bc([128, D]))
                kb.copy("dve", S32b.v, sstart[:, 1, :, :])
                kb.copy("act", SBb.v, sstart[:, 1, :, :])
                for t in range(NT - 1, -1, -1):
                    for hp in range(0, NH, 2):
                        gens = []
                        for i2 in range(2):
                            h = hp + i2
                            kb.dma("sp", qb[i2].v, s_qb[t, h])
                            kb.dma("sp", kbb[i2].v, s_kb[t, h])
                            kb.dma("sp", ebb[i2].v, s_eb[t, h])
                            kb.dma("sp", vb[i2].v.r("p a c -> p (a c)"), s_v[t, h])
                            kb.dma("sp", ofb[i2].v.r("p a c -> p (a c)"), s_of[t, h])
                            kb.dma("sp", sgb[i2].v.r("p a c -> p (a c)"), s_sg[t, h])
                            gens.append(scan_dir(s2, False, qb[i2].v, kbb[i2].v, vb[i2].v, ebb[i2].v, S32b[:, h, :],
                                                 SBb[:, h, :], tb if i2 == 0 else tb2, ofb[i2].v, ofb[i2].v, bset=i2))
                        rest = interleave(gens[0], gens[1])
                        drain(rest)
                        for i2 in range(2):
                            h = hp + i2
                            for sub in range(4):
                                kb.act(junk.v, ofb[i2][:, sub, :], AF.Square, accum=rst[:, sub:sub + 1])
                            kb.act(rst[:, 4:8], rst[:, 0:4], AF.Sqrt, scale=1.0 / 128.0, bias=epst[:, 0:1])
                            kb.op("dve", lambda e: e.reciprocal(out=rst.t[:, 8:12], in_=rst.t[:, 4:8]),
                                  reads=[rst], writes=[rst])
                            for sub in range(4):
                                kb.stt(Yt[:, sub, h * 128:(h + 1) * 128], ofb[i2][:, sub, :], rst[:, 8 + sub:9 + sub],
                                       sgb[i2][:, sub, :], ALU.mult, ALU.mult)
                    for sub in range(4):
                        ps = bank(4)
                        pv = bfv(ps)
                        for h in range(NH):
                            kb.tr(pv[:, h * 128:(h + 1) * 128], Yt[:, sub, h * 128:(h + 1) * 128], ident.v)
                        kb.copy("act" if sub % 2 else "dve", Yf[:, 0:8, sub * 128:(sub + 1) * 128],
                                pv.r("p (h c) -> p h c", h=8))
                    kb.dma("sp", Yf[:, 8:16, :], s_yc[t].r("c p n -> p c n"))
                    for sub in range(4):
                        kb.dma("sp", R1[sub].v, x_d[t * T + sub * 128:t * T + (sub + 1) * 128, :])
                    rg = Ring([(wo[g], KC * 512, ("p (k c) -> p k c", dict(k=KC))) for g in range(4)])
                    for g in range(4):
                        w_ = rg.get(g)[:, 0:KC * 512].r("p (k c) -> p k c", k=KC)
                        for sub in range(4):
                            ps = bank(4)
                            for kc in range(KC):
                                kb.mm(ps.v, Yf[:, kc, sub * 128:(sub + 1) * 128], w_[:, kc, :], kc == 0, kc == KC - 1)
                            kb.stt(R1[sub][:, g * 512:(g + 1) * 512], R1[sub][:, g * 512:(g + 1) * 512], ALPHA, ps.v,
                                   ALU.mult, ALU.add)
                    for sub in range(4):
                        r1 = R1[sub].v
                        ln_stats(r1, stat, epst)
                        kb.act(r1, r1, AF.Identity, scale=stat[:, 0:1], bias=stat[:, 1:2])
                        kb.tt("dve", r1, r1, g1rep.v, ALU.mult)
                        kb.tt("pool", r1, r1, b1rep.v, ALU.add)
                        kb.dma("pool", s_x1[t, sub], r1)
                        ln_stats(r1, stat, epst)
                        kb.act(xn2[sub % 2].v, r1, AF.Identity, scale=stat[:, 0:1], bias=stat[:, 1:2])
                        to_fm(xn2[sub % 2], H2, sub * 128, M_SCLF, M_SHFF)
                    kb.dma("pool", s_h2[t], H2.v)
                kb.barrier()

        if _DBG.get("stop", 0) == 0 or _DBG.get("stop", 0) >= 30:
         with ExitStack() as s3:
          if True:
                g2rep = kb.sbuf(s3, "g2rep", [128, D], F32)
                b2rep = kb.sbuf(s3, "b2rep", [128, D], F32)
                stat = kb.sbuf(s3, "stat3", [128, 32], F32)
                epst = kb.sbuf(s3, "epst3", [128, 1], F32)
                H2b = [kb.sbuf(s3, "H2b0", [128, KC, T], BF16)]
                A = kb.sbuf(s3, "A", [128, JF, T], BF16)
                sgt = [kb.sbuf(s3, "sgt%d" % i, [128, T], F32) for i in range(2)]
                R2 = [kb.sbuf(s3, "R2_%d" % i, [128, D], F32) for i in range(4)]
                kb.memset("dve", epst.v, EPS)
                kb.dma("sp", g2rep.v, ln2g_d[0:1, :].bc([128, D]))
                kb.dma("sp", b2rep.v, ln2b_d[0:1, :].bc([128, D]))
                for t in range(NT):
                    h2 = H2b[0]
                    kb.dma("sp", h2.v, s_h2[t])
                    items = [(wgu[jg], KC * 512, ("p (k c) -> p k c", dict(k=KC))) for jg in range(22)]
                    for g in range(4):
                        for pc in range(4):
                            items.append((wdn[g][:, pc * 11:(pc + 1) * 11, :], 11 * 512, ("p (j c) -> p j c", dict(j=11))))
                    rg = Ring(items)
                    for jg in range(22):
                        w_ = rg.get(jg)[:, 0:KC * 512].r("p (k jj gu c) -> p k jj gu c", k=KC, jj=2, gu=2)
                        for jj in range(2):
                            j = jg * 2 + jj
                            pg, pu = bank(8), bank(8)
                            for kc in range(KC):
                                kb.mm(pg.v, w_[:, kc, jj, 0, :], h2[:, kc, :], kc == 0, kc == KC - 1)
                            for kc in range(KC):
                                kb.mm(pu.v, w_[:, kc, jj, 1, :], h2[:, kc, :], kc == 0, kc == KC - 1)
                            kb.act(sgt[j % 2].v, pg.v, AF.Silu)
                            kb.tt("dve", A[:, j, :], pu.v, sgt[j % 2].v, ALU.mult)
                    for sub in range(4):
                        kb.dma("sp", R2[sub].v, s_x1[t, sub])
                    for g in range(4):
                        pss = [bank(8) for _ in range(4)]
                        for pc in range(4):
                            w_ = rg.get(22 + g * 4 + pc)[:, 0:11 * 512].r("p (j c) -> p j c", j=11)
                            for sub in range(4):
                                for jj in range(11):
                                    j = pc * 11 + jj
                                    kb.mm(pss[sub].v, A[:, j, sub * 128:(sub + 1) * 128], w_[:, jj, :], j == 0, j == JF - 1,
                                          signal=(jj == 10))
                        for sub in range(4):
                            kb.stt(R2[sub][:, g * 512:(g + 1) * 512], R2[sub][:, g * 512:(g + 1) * 512], ALPHA,
                                   pss[sub].v, ALU.mult, ALU.add)
                    for sub in range(4):
                        r2 = R2[sub].v
                        ln_stats(r2, stat, epst)
                        kb.act(r2, r2, AF.Identity, scale=stat[:, 0:1], bias=stat[:, 1:2])
                        kb.tt("dve", r2, r2, g2rep.v, ALU.mult)
                        kb.tt("pool", r2, r2, b2rep.v, ALU.add)
                        kb.dma("pool", out_d[t * T + sub * 128:t * T + (sub + 1) * 128, :], r2)
                kb.barrier()
        print("kernel: %d instructions emitted" % kb.ninst)
    _DBG["names"] = kb.names
    return nc


def make_in_maps(inp, NT):
    x = np.asarray(inp["x"], np.float32)
    B, N, _ = x.shape
    seg = NT * T
    nseg = N // seg
    ctx = np.asarray(inp["ctx"], np.float32)
    c = np.asarray(inp["c"], np.float32)
    c_ctx = np.asarray(inp["c_ctx"], np.float32)
    f32 = lambda a: np.ascontiguousarray(np.asarray(a, np.float32))
    lbl = f32(np.asarray(inp["lb_logits"]).reshape(2, 2, NH, 128).transpose(3, 0, 1, 2))
    convw = f32(np.asarray(inp["conv_w"])[0].reshape(3, NH, 128).transpose(2, 0, 1))
    bmod = f32(np.asarray(inp["b_mod"])[0])
    shared = {
        "w_mod": f32(inp["w_mod"][0]), "bmod_fm": f32(bmod.reshape(96, 128).T), "b_mod": f32(bmod[None, :]),
        "w_in": f32(inp["w_in"][0]), "lbl": lbl, "g_norm_w": f32(np.asarray(inp["g_norm_w"])[0][None, :]),
        "convw": convw, "w_out": f32(inp["w_out"][0]),
        "ln1_g": f32(np.asarray(inp["ln1_g"])[0][None, :]), "ln1_b": f32(np.asarray(inp["ln1_b"])[0][None, :]),
        "w_gate": f32(inp["w_gate"][0]), "w_up": f32(inp["w_up"][0]), "w_down": f32(inp["w_down"][0]),
        "ln2_g": f32(np.asarray(inp["ln2_g"])[0][None, :]), "ln2_b": f32(np.asarray(inp["ln2_b"])[0][None, :]),
    }
    maps = []
    for r in range(B * nseg):
        b, j = divmod(r, nseg)
        t0 = j * seg
        m = dict(shared)
        m["x"] = f32(x[b, t0:t0 + seg])
        xhf = np.zeros((T, D), np.float32)
        xhb = np.zeros((T, D), np.float32)
        vmf = np.ones(T, np.float32)
        vmb = np.ones(T, np.float32)
        sel = np.zeros((128, 2), np.float32)
        if j == 0:
            xhf[T - CTX:] = ctx[b]
            vmf[:T - CTX] = 0.0
            sel[:, 0] = 1.0
        else:
            xhf[:] = x[b, t0 - T:t0]
        if j == nseg - 1:
            xhb[:CTX] = ctx[b]
            vmb[CTX:] = 0.0
            sel[:, 1] = 1.0
        else:
            xhb[:] = x[b, t0 + seg:t0 + seg + T]
        m["xhf"], m["xhb"], m["sel"] = xhf, xhb, sel
        m["vm"] = f32(np.concatenate([vmf.reshape(4, 128).T, vmb.reshape(4, 128).T], axis=1))
        cv = np.stack([c[b].reshape(KC, 128).T, c_ctx.reshape(KC, 128).T], axis=2)
        m["cvec"] = f32(cv)
        maps.append(m)
    return maps, B, nseg, seg


_NC_CACHE = {}


def kernel(**inputs):
    x = np.asarray(inputs["x"])
    B, N, _ = x.shape
    NT = N // (4 * T)
    maps, B, nseg, seg = make_in_maps(inputs, NT)
    if NT not in _NC_CACHE:
        _NC_CACHE[NT] = build_program(NT)
    nc = _NC_CACHE[NT]
    res = run_bass_kernel_spmd(nc, maps, core_ids=list(range(len(maps))))
    out = np.empty((B, N, D), np.float32)
    for r in range(B * nseg):
        b, j = divmod(r, nseg)
        out[b, j * seg:(j + 1) * seg] = res.results[r]["out"]
    return out
```

```python
import numpy as np
import concourse.bass as bass
import concourse.mybir as mybir
from concourse.bass_utils import run_bass_kernel_spmd
from contextlib import ExitStack

F32 = mybir.dt.float32
BF16 = mybir.dt.bfloat16
AF = mybir.ActivationFunctionType
ALU = mybir.AluOpType

D = 2048
KC = 16
NH = 8
DFF = 5632
JF = 44
T = 512
CTX = 256
ALPHA = 2.0 ** 0.25
EPS = 1e-6


class View:
    __slots__ = ("b", "ap")

    def __init__(self, b, ap):
        self.b = b
        self.ap = ap

    def __getitem__(self, idx):
        return View(self.b, self.ap[idx])

    def r(self, pat, **kw):
        return View(self.b, self.ap.rearrange(pat, **kw))

    def bc(self, shape):
        return View(self.b, self.ap.broadcast_to(list(shape)))

    def un(self, d):
        return View(self.b, self.ap.unsqueeze(d))


class Buf:
    __slots__ = ("t", "w", "r", "name", "dcnt", "kind")

    def __init__(self, t, name, kind):
        self.t = t
        self.name = name
        self.kind = kind
        self.w = {}
        self.r = {}
        self.dcnt = 0

    def __getitem__(self, idx):
        return View(self, self.t[idx])

    @property
    def v(self):
        return View(self, self.t[:])


class KB:
    def __init__(self, nc, es):
        self.nc = nc
        self.es = es
        self.eng = {"pe": nc.tensor, "act": nc.scalar, "dve": nc.vector,
                    "pool": nc.gpsimd, "sp": nc.sync}
        self.sems = {}
        self.cnt = {}
        self.waited = {}
        self.keycnt = {}
        for e in self.eng:
            self.sems[e] = es.enter_context(nc.semaphore("s_" + e))
            self.cnt[e] = 0
            self.waited[e] = {}
        self.ninst = 0
        self.uid = 0
        self.names = {}

    def sbuf(self, es, name, shape, dt):
        self.uid += 1
        t = es.enter_context(self.nc.sbuf_tensor("%s_%d" % (name, self.uid), list(shape), dt))
        self.names[name] = "%s_%d" % (name, self.uid)
        return Buf(t, name, "sb")

    def psum(self, es, name, shape, dt):
        t = es.enter_context(self.nc.psum_tensor(name, list(shape), dt))
        return Buf(t, name, "ps")

    def dram(self, name, shape, dt, kind="Internal"):
        t = self.nc.dram_tensor(name, list(shape), dt, kind=kind)
        return Buf(t.ap(), name, "dr")

    def _key(self, b, q="sp"):
        k = "d_" + b.name + ("_sw" if q == "pool" else "")
        if k not in self.sems:
            self.sems[k] = self.es.enter_context(self.nc.semaphore(k))
            self.keycnt[k] = 0
        return k

    def _waits(self, e, reads, writes):
        need = {}
        for b in reads:
            for k, v in b.w.items():
                if need.get(k, 0) < v:
                    need[k] = v
        for b in writes:
            for k, v in b.w.items():
                if need.get(k, 0) < v:
                    need[k] = v
            for k, v in b.r.items():
                if need.get(k, 0) < v:
                    need[k] = v
        wd = self.waited[e]
        for k, v in need.items():
            if e == "pe" and k == "pe":
                continue
            if wd.get(k, 0) < v:
                self.eng[e].wait_ge(self.sems[k], v)
                wd[k] = v

    def op(self, e, fn, reads=(), writes=(), quiet=False):
        if _DBG.get("halt"):
            return None
        ps_r = [b for b in reads if b.kind == "ps"]
        if ps_r:
            writes = list(writes) + ps_r
        self._waits(e, reads, writes)
        inst = fn(self.eng[e])
        self.ninst += 1
        if quiet:
            tok = self.cnt[e] + 1
        else:
            self.cnt[e] += 1
            tok = self.cnt[e]
            inst.then_inc(self.sems[e], 1)
        for b in reads:
            if b.r.get(e, 0) < tok:
                b.r[e] = tok
        for b in writes:
            if b.w.get(e, 0) < tok:
                b.w[e] = tok
        return inst

    def dma(self, q, out, in_, key=None):
        if _DBG.get("halt"):
            return
        src, dst = in_.b, out.b
        if key is None:
            key = dst if dst.kind == "sb" else (src if src.kind == "sb" else dst)
        self._waits(q, [src], [dst] if dst.kind != "dr" else [])
        k = self._key(key, q)
        inst = self.eng[q].dma_start(out=out.ap, in_=in_.ap)
        self.keycnt[k] += 16
        inst.then_inc(self.sems[k], 16)
        self.ninst += 1
        v = self.keycnt[k]
        if src.kind != "dr":
            src.r[k] = max(src.r.get(k, 0), v)
        dst.w[k] = max(dst.w.get(k, 0), v)

    def barrier(self):
        for e in self.eng:
            wd = self.waited[e]
            for k in list(self.sems.keys()):
                v = self.cnt[k] if k in self.cnt else self.keycnt[k]
                if k == e or v == 0:
                    continue
                if wd.get(k, 0) < v:
                    self.eng[e].wait_ge(self.sems[k], v)
                    wd[k] = v

    @staticmethod
    def _bufs(*vs):
        return [v.b for v in vs if isinstance(v, View)]

    @staticmethod
    def _a(v):
        return v.ap if isinstance(v, View) else v

    def mm(self, out, lhsT, rhs, start, stop, signal=False):
        self.op("pe", lambda e: e.matmul(out.ap, lhsT=lhsT.ap, rhs=rhs.ap, start=start, stop=stop),
                reads=[lhsT.b, rhs.b], writes=[out.b], quiet=not (stop or signal))

    def tr(self, out, in_, ident):
        self.op("pe", lambda e: e.transpose(out=out.ap, in_=in_.ap, identity=ident.ap),
                reads=[in_.b, ident.b], writes=[out.b])

    def act(self, out, in_, func, scale=1.0, bias=0.0, accum=None):
        if isinstance(scale, View) and not isinstance(bias, View):
            bias = self.cst[:, 0:1] if bias == 0.0 else self.cst[:, 1:2]
            if func == AF.Copy:
                func = AF.Identity
        rd = self._bufs(in_, scale, bias)
        wr = [out.b] + ([accum.b] if accum is not None else [])
        kw = {}
        if accum is not None:
            kw["accum_out"] = accum.ap
        self.op("act", lambda e: e.activation(out=out.ap, in_=in_.ap, func=func, scale=self._a(scale),
                                              bias=self._a(bias), **kw), reads=rd, writes=wr)

    def tt(self, eng, out, in0, in1, op):
        self.op(eng, lambda e: e.tensor_tensor(out=out.ap, in0=in0.ap, in1=in1.ap, op=op),
                reads=[in0.b, in1.b], writes=[out.b])

    def ts(self, eng, out, in0, s1, op0, s2=None, op1=None):
        rd = self._bufs(in0, s1, s2)
        if op1 is None:
            self.op(eng, lambda e: e.tensor_scalar(out=out.ap, in0=in0.ap, scalar1=self._a(s1), scalar2=None,
                                                   op0=op0), reads=rd, writes=[out.b])
        else:
            self.op(eng, lambda e: e.tensor_scalar(out=out.ap, in0=in0.ap, scalar1=self._a(s1),
                                                   scalar2=self._a(s2), op0=op0, op1=op1), reads=rd, writes=[out.b])

    def stt(self, out, in0, scalar, in1, op0, op1):
        rd = self._bufs(in0, scalar, in1)
        self.op("dve", lambda e: e.scalar_tensor_tensor(out=out.ap, in0=in0.ap, scalar=self._a(scalar),
                                                        in1=in1.ap, op0=op0, op1=op1), reads=rd, writes=[out.b])

    def copy(self, eng, out, in_):
        if eng == "act":
            self.act(out, in_, AF.Copy)
        else:
            self.op(eng, lambda e: e.tensor_copy(out=out.ap, in_=in_.ap), reads=[in_.b], writes=[out.b])

    def memset(self, eng, out, val):
        self.op(eng, lambda e: e.memset(out.ap, val), writes=[out.b])


_DBG = {}


class StopBuild(Exception):
    pass


def chk(n):
    if _DBG.get("stop") == n:
        _DBG["halt"] = True

def build_program(NT):
    _DBG["halt"] = False
    nc = bass.Bass("TRN2", target_bir_lowering=False)
    NTOK = NT * T
    ges = ExitStack()
    with ges:
        kb = KB(nc, ges)

        def ext_in(name, shape, dt=F32):
            if _DBG.get("noweights") and name in ("w_mod", "w_in", "w_out", "w_gate", "w_up", "w_down"):
                return kb.dram(name, shape, dt)
            return Buf(nc.dram_tensor(name, list(shape), dt, kind="ExternalInput").ap(), name, "dr")

        x_d = ext_in("x", [NTOK, D])
        xhf_d = ext_in("xhf", [T, D])
        xhb_d = ext_in("xhb", [T, D])
        vm_d = ext_in("vm", [128, 8])
        sel_d = ext_in("sel", [128, 2])
        cvec_d = ext_in("cvec", [128, KC, 2])
        wmod_d = ext_in("w_mod", [D, 6 * D])
        bmodfm_d = ext_in("bmod_fm", [128, 96])
        bmod_d = ext_in("b_mod", [1, 6 * D])
        win_d = ext_in("w_in", [D, 8192])
        lbl_d = ext_in("lbl", [128, 2, 2, NH])
        gw_d = ext_in("g_norm_w", [1, 128])
        cw_d = ext_in("convw", [128, 3, NH])
        wout_d = ext_in("w_out", [D, D])
        ln1g_d = ext_in("ln1_g", [1, D])
        ln1b_d = ext_in("ln1_b", [1, D])
        wg_d = ext_in("w_gate", [D, DFF])
        wu_d = ext_in("w_up", [D, DFF])
        wd_d = ext_in("w_down", [DFF, D])
        ln2g_d = ext_in("ln2_g", [1, D])
        ln2b_d = ext_in("ln2_b", [1, D])
        out_d = Buf(nc.dram_tensor("out", [NTOK, D], F32, kind="ExternalOutput").ap(), "out", "dr")

        wfm = kb.dram("wfm", [16, 128, KC, 384], BF16)
        wtm = kb.dram("wtm", [NH, 128, KC, 256], BF16)
        wo = kb.dram("wo", [4, 128, KC, 512], BF16)
        wgu = kb.dram("wgu", [22, 128, KC, 512], BF16)
        wdn = kb.dram("wdn", [4, 128, JF, 512], BF16)
        s_v = kb.dram("s_v", [NT, NH, 128, 512], BF16)
        s_sg = kb.dram("s_sg", [NT, NH, 128, 512], F32)
        s_of = kb.dram("s_of", [NT, NH, 128, 512], F32)
        s_qb = kb.dram("s_qb", [NT, NH, 128, 768], BF16)
        s_kb = kb.dram("s_kb", [NT, NH, 128, 512], BF16)
        s_eb = kb.dram("s_eb", [NT, NH, 128, 8], F32)
        s_yc = kb.dram("s_yc", [NT, NH, 128, 512], BF16)
        s_x1 = kb.dram("s_x1", [NT, 4, 128, D], F32)
        s_h2 = kb.dram("s_h2", [NT, 128, KC, 512], BF16)

        pb = [kb.psum(ges, "pb%d" % i, [128, 512], F32) for i in range(8)]
        rot = {"i": 0}

        def bank(n=4, base=0):
            i = rot["i"] % n
            rot["i"] = (i + 1) % n
            return pb[base + i]

        def bfv(b):
            return View(b, b.t[:].bitcast(BF16))

        ident = kb.sbuf(ges, "ident", [128, 128], BF16)
        identf = kb.sbuf(ges, "identf", [128, 128], F32)
        maskF = kb.sbuf(ges, "maskF", [128, 128], F32)
        maskB = kb.sbuf(ges, "maskB", [128, 128], F32)
        rmask = kb.sbuf(ges, "rmask", [128, T], F32)
        modv = kb.sbuf(ges, "modv", [128, 12, KC], F32)
        oml = kb.sbuf(ges, "oml", [128, 2, 2, NH], F32)
        gwrep = kb.sbuf(ges, "gwrep", [128, 128], F32)
        convw = kb.sbuf(ges, "convw", [128, 3, NH], F32)
        vm = kb.sbuf(ges, "vm", [128, 8], F32)
        sel = kb.sbuf(ges, "sel", [128, 2], F32)
        sstart = kb.sbuf(ges, "sstart", [128, 2, NH, 128], F32)
        ring = [kb.sbuf(ges, "ring%d" % i, [128, 8192], BF16) for i in range(3)]
        M_SCLA, M_SHFA, M_SCLF, M_SHFF, M_SCLC, M_SHFC, M_SCLHF, M_SHFHF, M_SCLHB, M_SHFHB = range(10)

        cst = kb.sbuf(ges, "cst", [128, 2], F32)
        kb.memset("dve", cst[:, 0:1], 0.0)
        kb.memset("dve", cst[:, 1:2], 1.0)
        kb.cst = cst
        kb.memset("pool", identf.v, 1.0)
        kb.op("pool", lambda e: e.affine_select(out=identf.t[:], in_=identf.t[:], pattern=[[-1, 128]],
                                                compare_op=ALU.is_equal, fill=0.0, base=0, channel_multiplier=1),
              reads=[identf], writes=[identf])
        kb.copy("dve", ident.v, identf.v)
        kb.memset("pool", maskF.v, 1.0)
        kb.op("pool", lambda e: e.affine_select(out=maskF.t[:], in_=maskF.t[:], pattern=[[1, 128]],
                                                compare_op=ALU.is_ge, fill=0.0, base=0, channel_multiplier=-1),
              reads=[maskF], writes=[maskF])
        kb.memset("pool", maskF[0:64, 64:128], 0.0)
        kb.memset("pool", maskB.v, 1.0)
        kb.op("pool", lambda e: e.affine_select(out=maskB.t[:], in_=maskB.t[:], pattern=[[-1, 128]],
                                                compare_op=ALU.is_ge, fill=0.0, base=0, channel_multiplier=1),
              reads=[maskB], writes=[maskB])
        kb.memset("pool", maskB[64:128, 0:64], 0.0)
        kb.memset("dve", rmask.v, 1.0)
        kb.memset("dve", rmask.v.r("p (c k) -> p c k", k=64)[:, :, 0:1], 0.0)
        kb.dma("sp", gwrep.v, gw_d[0:1, :].bc([128, 128]))
        kb.dma("sp", convw.v, cw_d.v)
        kb.dma("sp", vm.v, vm_d.v)
        kb.dma("sp", sel.v, sel_d.v)

        with ExitStack() as ses0:
            garep = kb.sbuf(ses0, "garep", [128, 2, D], F32)
            with ExitStack() as ses:
                cvec = kb.sbuf(ses, "cvec", [128, KC, 2], F32)
                scv = kb.sbuf(ses, "scv", [128, KC, 2], F32)
                screp = kb.sbuf(ses, "screp", [128, KC, 128], F32)
                bmodfm = kb.sbuf(ses, "bmodfm", [128, 96], F32)
                modfm = kb.sbuf(ses, "modfm", [128, 96, 2], F32)
                lbl = kb.sbuf(ses, "lbl", [128, 2, 2, NH], F32)
                wmt = [kb.sbuf(ses, "wmt%d" % i, [128, KC, 512], F32) for i in range(2)]

                kb.dma("sp", cvec.v, cvec_d.v)
                kb.dma("sp", garep[:, 0, :], bmod_d[0:1, 2 * D:3 * D].bc([128, D]))
                kb.dma("sp", garep[:, 1, :], bmod_d[0:1, 5 * D:6 * D].bc([128, D]))
                kb.dma("sp", bmodfm.v, bmodfm_d.v)
                kb.dma("sp", lbl.v, lbl_d.v)
                kb.act(scv.v, cvec.v, AF.Silu)
                kb.copy("dve", screp.v, scv[:, :, 0:1].bc([128, KC, 128]))
                kb.tt("dve", lbl[:, :, 0, :], lbl[:, :, 1, :], lbl[:, :, 0, :], ALU.subtract)
                kb.act(oml[:, :, 0, :], lbl[:, :, 0, :], AF.Sigmoid)
                kb.ts("dve", oml[:, :, 1, :], oml[:, :, 0, :], -1.0, ALU.mult)

                wm_v = wmod_d.v.r("(kc p) n -> p kc n", p=128)
                for blk in range(24):
                    fam = blk // 4
                    wt_ = wmt[blk % 2]
                    kb.dma("sp", wt_.v, wm_v[:, :, blk * 512:(blk + 1) * 512])
                    if fam in (2, 5):
                        g = 0 if fam == 2 else 1
                        col = (blk % 4) * 512
                        ps = bank(8)
                        for kc in range(KC):
                            kb.mm(ps.v, screp[:, kc, :], wt_[:, kc, :], kc == 0, kc == KC - 1)
                        kb.tt("dve", garep[:, g, col:col + 512], ps.v, garep[:, g, col:col + 512], ALU.add)
                    else:
                        ps = bank(8)
                        for cc in range(4):
                            for kc in range(KC):
                                kb.mm(ps[:, cc * 2:cc * 2 + 2], wt_[:, kc, cc * 128:(cc + 1) * 128], scv[:, kc, :],
                                      kc == 0, kc == KC - 1)
                        kb.tt("dve", modfm[:, blk * 4:(blk + 1) * 4, :],
                              ps[:, 0:8].r("p (c t) -> p c t", t=2),
                              bmodfm[:, blk * 4:(blk + 1) * 4].un(2).bc([128, 4, 2]), ALU.add)

                def mfam(f, which):
                    return modfm[:, f * KC:(f + 1) * KC, which]

                kb.ts("dve", modv[:, M_SCLA, :], mfam(1, 0), 1.0, ALU.add)
                kb.copy("dve", modv[:, M_SHFA, :], mfam(0, 0))
                kb.ts("dve", modv[:, M_SCLF, :], mfam(4, 0), 1.0, ALU.add)
                kb.copy("dve", modv[:, M_SHFF, :], mfam(3, 0))
                kb.ts("dve", modv[:, M_SCLC, :], mfam(1, 1), 1.0, ALU.add)
                kb.copy("dve", modv[:, M_SHFC, :], mfam(0, 1))
                for (dst_s, dst_h, sc) in ((M_SCLHF, M_SHFHF, 0), (M_SCLHB, M_SHFHB, 1)):
                    for (dst, own, cx) in ((dst_s, M_SCLA, M_SCLC), (dst_h, M_SHFA, M_SHFC)):
                        kb.tt("dve", modv[:, dst, :], modv[:, cx, :], modv[:, own, :], ALU.subtract)
                        kb.stt(modv[:, dst, :], modv[:, dst, :], sel[:, sc:sc + 1], modv[:, own, :],
                               ALU.mult, ALU.add)
                kb.barrier()

            with ExitStack() as ses:
              if _DBG.get("stop") != 5:
                    csrc = [kb.sbuf(ses, "csrc%d" % i, [128, 8192], F32) for i in range(2)]
                    cdst = [kb.sbuf(ses, "cdst%d" % i, [128, 8192], BF16) for i in range(2)]
                    engs = ["act", "dve", "pool"]
                    blocks = []

                    for kc in range(KC):
                        def loads(s_, kc=kc):
                            return [(s_.v, win_d[kc * 128:(kc + 1) * 128, :])]

                        def casts(s_, d_, i):
                            sv = s_.v.r("p (f h c) -> p f h c", f=8, h=NH)
                            fm = d_[:, 0:6144].r("p (u g c) -> p u g c", u=16, g=3)
                            tmv = d_[:, 6144:8192].r("p (h g c) -> p h g c", h=NH, g=2)
                            plan = [(fm[:, 0:8, 0, :], 0), (fm[:, 0:8, 1, :], 1), (fm[:, 0:8, 2, :], 3),
                                    (fm[:, 8:16, 0, :], 5), (fm[:, 8:16, 1, :], 6), (fm[:, 8:16, 2, :], 7),
                                    (tmv[:, :, 0, :], 2), (tmv[:, :, 1, :], 4)]
                            for n, (dv, f) in enumerate(plan):
                                kb.copy(engs[(n + i) % 3], dv, sv[:, f, :, :])

                        def stores(d_, kc=kc):
                            return [(wfm[:, :, kc, :].r("u p c -> p u c"), d_[:, 0:6144].r("p (u c) -> p u c", u=16)),
                                    (wtm[:, :, kc, :].r("h p c -> p h c"), d_[:, 6144:8192].r("p (h c) -> p h c", h=NH))]
                        blocks.append((loads, casts, stores))
                    for kc in range(KC):
                        def loads(s_, kc=kc):
                            return [(s_[:, 0:2048], wout_d[kc * 128:(kc + 1) * 128, :])]

                        def casts(s_, d_, i):
                            for q in range(4):
                                kb.tt(["dve", "pool"][q % 2], d_[:, q * 512:(q + 1) * 512], s_[:, q * 512:(q + 1) * 512],
                                      garep[:, 0, q * 512:(q + 1) * 512], ALU.mult)

                        def stores(d_, kc=kc):
                            return [(wo[:, :, kc, :].r("g p c -> p g c"), d_[:, 0:2048].r("p (g c) -> p g c", g=4))]
                        blocks.append((loads, casts, stores))
                    for kc in range(KC):
                        for hf in range(2):
                            def loads(s_, kc=kc, hf=hf):
                                c0 = hf * 2816
                                return [(s_[:, 0:2816], wg_d[kc * 128:(kc + 1) * 128, c0:c0 + 2816]),
                                        (s_[:, 2816:5632], wu_d[kc * 128:(kc + 1) * 128, c0:c0 + 2816])]

                            def casts(s_, d_, i):
                                dv = d_[:, 0:5632].r("p (j gu c) -> p j gu c", j=22, gu=2)
                                kb.copy(engs[i % 3], dv[:, :, 0, :], s_[:, 0:2816].r("p (j c) -> p j c", j=22))
                                kb.copy(engs[(i + 1) % 3], dv[:, :, 1, :], s_[:, 2816:5632].r("p (j c) -> p j c", j=22))

                            def stores(d_, kc=kc, hf=hf):
                                return [(wgu[hf * 11:(hf + 1) * 11, :, kc, :].r("jg p c -> p jg c"),
                                         d_[:, 0:5632].r("p (jg c) -> p jg c", jg=11))]
                            blocks.append((loads, casts, stores))
                    for jb in range(JF // 4):
                        def loads(s_, jb=jb):
                            return [(s_.v.r("p (j n) -> p j n", j=4),
                                     wd_d[jb * 512:(jb + 1) * 512, :].r("(j p) n -> p j n", p=128))]

                        def casts(s_, d_, i):
                            for q in range(4):
                                kb.tt(["dve", "pool"][q % 2], d_[:, q * 2048:(q + 1) * 2048],
                                      s_[:, q * 2048:(q + 1) * 2048], garep[:, 1, :], ALU.mult)

                        def stores(d_, jb=jb):
                            dv4 = d_.v.r("p (j g c) -> p j g c", j=4, g=4)
                            return [(wdn[g, :, jb * 4:(jb + 1) * 4, :], dv4[:, :, g, :]) for g in range(4)]
                        blocks.append((loads, casts, stores))

                    def do_loads(i):
                        for (dv, sv) in blocks[i][0](csrc[i % 2]):
                            kb.dma("sp", dv, sv)
                    do_loads(0)
                    for i in range(len(blocks)):
                        if i + 1 < len(blocks):
                            do_loads(i + 1)
                        blocks[i][1](csrc[i % 2], cdst[i % 2], i)
                        for (dv, sv) in blocks[i][2](cdst[i % 2]):
                            kb.dma("pool", dv, sv)
                    kb.barrier()

        def ln_stats(xin, stat, eps_t):
            for q in range(4):
                kb.op("dve", lambda e, q=q: e.bn_stats(out=stat.t[:, 8 + q * 6:14 + q * 6],
                                                       in_=xin.ap[:, q * 512:(q + 1) * 512]),
                      reads=[xin.b], writes=[stat])
            kb.op("dve", lambda e: e.bn_aggr(out=stat.t[:, 2:4], in_=stat.t[:, 8:32]), reads=[stat], writes=[stat])
            kb.act(stat[:, 4:5], stat[:, 3:4], AF.Sqrt, bias=eps_t[:, 0:1])
            kb.op("dve", lambda e: e.reciprocal(out=stat.t[:, 0:1], in_=stat.t[:, 4:5]), reads=[stat], writes=[stat])
            kb.stt(stat[:, 1:2], stat[:, 2:3], -1.0, stat[:, 0:1], ALU.mult, ALU.mult)

        def to_fm(xn, hdst, col0, scl_i, shf_i):
            for half in range(2):
                ps = bank(4)
                pv = bfv(ps)
                for k8 in range(8):
                    kc = half * 8 + k8
                    kb.tr(pv[:, k8 * 128:(k8 + 1) * 128], xn[:, kc * 128:(kc + 1) * 128], ident.v)
                for k8 in range(8):
                    kc = half * 8 + k8
                    if k8 % 2 == 0:
                        kb.ts("dve", hdst[:, kc, col0:col0 + 128], pv[:, k8 * 128:(k8 + 1) * 128],
                              modv[:, scl_i, kc:kc + 1], ALU.mult, modv[:, shf_i, kc:kc + 1], ALU.add)
                    else:
                        kb.act(hdst[:, kc, col0:col0 + 128], pv[:, k8 * 128:(k8 + 1) * 128], AF.Identity,
                               scale=modv[:, scl_i, kc:kc + 1], bias=modv[:, shf_i, kc:kc + 1])

        class Ring:
            def __init__(self, items, hold=1):
                self.hold = hold
                self.items = items
                self.issued = 0

            def get(self, i):
                while self.issued < min(len(self.items), i + 4 - self.hold):
                    n = self.issued
                    dv, ncols, pat = self.items[n]
                    slot = ring[n % 3]
                    sv = slot[:, 0:ncols]
                    if pat is not None:
                        sv = sv.r(pat[0], **pat[1])
                    kb.dma("sp", sv, dv)
                    self.issued += 1
                return ring[i % 3]

        scan_rot = {"i": 0}

        def scan_dir(es_s, fwd, Qp, Kt, Vh, Eend, S32, SB, tmpb, O_out, add_to, bset=None):
            KT_A, KT_B, PT, sbf, tmp = tmpb["KT_A"], tmpb["KT_B"], tmpb["PT"], tmpb["sbf"], tmpb["tmp"]
            mask = maskF if fwd else maskB
            pairs = list(range(4)) if fwd else list(range(3, -1, -1))
            base = 4 if (bset is None or bset == 0) else 0
            B_sc, B_tr, B_u0, B_u1 = pb[base], pb[base + 1], pb[base + 2], pb[base + 3]
            trv = bfv(B_tr)
            for pi, p in enumerate(pairs):
                ksl = Kt[:, p * 128:(p + 1) * 128]
                if Qp is not None:
                    qpair = Qp[:, p * 192:(p + 1) * 192].r("p (a c) -> p a c", c=64)[:, 0:3:2, :]
                    kb.mm(B_sc[:, pi * 128:(pi + 1) * 128], ksl, qpair, True, True)
                kb.tr(trv[:, pi * 128:(pi + 1) * 128], ksl, ident.v)
            for pi, p in enumerate(pairs):
                if Qp is not None:
                    kb.tt("dve", PT[pi].v, B_sc[:, pi * 128:(pi + 1) * 128], mask.v, ALU.mult)
                kb.copy("act", KT_A[pi][0:64, :], trv[0:64, pi * 128:(pi + 1) * 128])
                kb.copy("act", KT_B[pi][64:128, :], trv[64:128, pi * 128:(pi + 1) * 128])
            yield
            seq = []
            for pi, p in enumerate(pairs):
                order = ((KT_A, 2 * p), (KT_B, 2 * p + 1)) if fwd else ((KT_B, 2 * p + 1), (KT_A, 2 * p))
                for ci_, (KTx, ch) in enumerate(order):
                    j = 2 * pi + ci_
                    ups = (B_u0 if j < 4 else B_u1)[:, (j % 4) * 128:(j % 4 + 1) * 128]
                    kb.mm(ups, KTx[pi].v, Vh[:, p, :], True, True)
                    seq.append((ups, ch))
                yield
            states = [SB]
            for j, (ups, ch) in enumerate(seq):
                tj = tmp[j % 2]
                kb.tt("dve", tj.v, ups, S32, ALU.add)
                kb.ts("dve", S32, tj.v, Eend[:, ch:ch + 1], ALU.mult)
                kb.act(sbf[j].v, tj.v, AF.Copy, scale=Eend[:, ch:ch + 1])
                states.append(sbf[j].v)
                if j % 2 == 1:
                    yield
            if Qp is not None:
                for pi, p in enumerate(pairs):
                    ops_ = B_sc[:, pi * 128:(pi + 1) * 128]
                    q_lo = Qp[:, p * 192:p * 192 + 128]
                    q_hi = Qp[:, p * 192 + 64:p * 192 + 192]
                    q1, q2 = (q_lo, q_hi) if fwd else (q_hi, q_lo)
                    kb.mm(ops_, PT[pi].v, Vh[:, p, :], True, False)
                    kb.mm(ops_, q1, states[2 * pi], False, False)
                    kb.mm(ops_, q2, states[2 * pi + 1], False, True)
                    if add_to is None:
                        kb.copy("act", O_out[:, p, :], ops_)
                    else:
                        kb.tt("dve", O_out[:, p, :], ops_, add_to[:, p, :], ALU.add)
                    yield
            kb.copy("act", SB, states[8])

        def drain(g):
            if g is not None:
                for _ in g:
                    pass

        def interleave(main, filler, k=1, first=None):
            n_y = 0
            for _ in main:
                n_y += 1
                if filler is not None:
                    for _i in range(first if (first and n_y == 1) else k):
                        try:
                            next(filler)
                        except StopIteration:
                            filler = None
                            break
            return filler

        def scan_tmps(es_s):
            tb = {"KT_A": [kb.sbuf(es_s, "KT_A%d" % i, [128, 128], BF16) for i in range(4)],
                  "KT_B": [kb.sbuf(es_s, "KT_B%d" % i, [128, 128], BF16) for i in range(4)],
                  "PT": [kb.sbuf(es_s, "PT%d" % i, [128, 128], BF16) for i in range(4)],
                  "tmp": [kb.sbuf(es_s, "stmp%d" % i, [128, 128], F32) for i in range(2)],
                  "sbf": [kb.sbuf(es_s, "sbf%d" % i, [128, 128], BF16) for i in range(8)]}
            for i in range(4):
                kb.memset("pool", tb["KT_A"][i].v, 0.0)
                kb.memset("pool", tb["KT_B"][i].v, 0.0)
            return tb

        try:
         with ExitStack() as s1:
           if _DBG.get("stop") not in (4, 5):
                 xs = [kb.sbuf(s1, "xs%d" % i, [128, D], F32) for i in range(2)]
                 xn = [kb.sbuf(s1, "xn%d" % i, [128, D], BF16) for i in range(2)]
                 stat = kb.sbuf(s1, "stat", [128, 32], F32)
                 epst = kb.sbuf(s1, "epst", [128, 1], F32)
                 hT = kb.sbuf(s1, "hT", [128, KC, T], BF16)
                 Vh = [kb.sbuf(s1, "Vh%d" % i, [128, 4, 128], BF16) for i in range(3)]
                 SGh = [kb.sbuf(s1, "SGh%d" % i, [128, 4, 128], F32) for i in range(3)]
                 Ebf = [kb.sbuf(s1, "Ebf%d" % i, [128, 8], F32) for i in range(2)]
                 Oh = [kb.sbuf(s1, "Oh%d" % i, [128, 4, 128], F32) for i in range(2)]
                 qs = kb.sbuf(s1, "qs", [128, T], F32)
                 sg_ = [kb.sbuf(s1, "sg%d" % i, [128, T], F32) for i in range(2)]
                 kk_ = [kb.sbuf(s1, "kk%d" % i, [128, T], F32) for i in range(2)]
                 lf_ = [kb.sbuf(s1, "lf%d" % i, [128, T], F32) for i in range(2)]
                 bb_ = [kb.sbuf(s1, "bb%d" % i, [128, T], F32) for i in range(2)]
                 e1_ = [kb.sbuf(s1, "e1%d" % i, [128, T], F32) for i in range(2)]
                 e2_ = [kb.sbuf(s1, "e2%d" % i, [128, T], F32) for i in range(2)]
                 Qp_ = [kb.sbuf(s1, "Qp%d" % i, [128, 768], BF16) for i in range(4)]
                 Kt_ = [kb.sbuf(s1, "Kt%d" % i, [128, T], BF16) for i in range(4)]
                 Eb_ = [kb.sbuf(s1, "Eb%d" % i, [128, 8], F32) for i in range(2)]
                 S32f = kb.sbuf(s1, "S32f", [128, NH, 128], F32)
                 SBf = kb.sbuf(s1, "SBf", [128, NH, 128], BF16)
                 S32h = kb.sbuf(s1, "S32h", [128, NH, 128], F32)
                 SBh = kb.sbuf(s1, "SBh", [128, NH, 128], BF16)
                 xv_s = kb.sbuf(s1, "xv_s", [128, T], F32)
                 zc = kb.sbuf(s1, "zc", [128, T], F32)
                 acc = kb.sbuf(s1, "acc", [128, T], F32)
                 yc = [kb.sbuf(s1, "yc%d" % i, [128, T], BF16) for i in range(2)]
                 tb = scan_tmps(s1)
                 kb.memset("dve", epst.v, EPS)
                 for q_ in Qp_:
                     kb.memset("pool", q_.v, 0.0)
                 chk(10)

                 def ln_tile(src_rows, scl_i, shf_i):
                     for sub in range(4):
                         xb_, xnb = xs[sub % 2], xn[sub % 2]
                         kb.dma("sp", xb_.v, src_rows(sub))
                         ln_stats(xb_.v, stat, epst)
                         kb.act(xnb.v, xb_.v, AF.Identity, scale=stat[:, 0:1], bias=stat[:, 1:2])
                         to_fm(xnb, hT, sub * 128, scl_i, shf_i)

                 def gates(fps, d, h, i2):
                     sg, kk, lf, bb, e1, e2 = sg_[i2], kk_[i2], lf_[i2], bb_[i2], e1_[i2], e2_[i2]
                     kb.act(sg.v, fps, AF.Sigmoid, scale=-1.0)
                     kb.ts("dve", kk.v, sg.v, oml[:, d, 0, h:h + 1], ALU.mult)
                     kb.act(lf.v, sg.v, AF.Ln, scale=oml[:, d, 1, h:h + 1], bias=1.0)
                     kb.op("dve", lambda e: e.tensor_tensor_scan(out=bb.t[:], data0=rmask.t[:], data1=lf.t[:], initial=0.0,
                                                                 op0=ALU.mult, op1=ALU.add),
                           reads=[rmask, lf], writes=[bb])
                     cum = bb
                     if d == 1:
                         kb.tt("pool", lf.v, lf.v, bb.v, ALU.subtract)
                         kb.tt("pool", sg.v.r("p (c k) -> p c k", k=64), lf.v.r("p (c k) -> p c k", k=64),
                               bb.v.r("p (c k) -> p c k", k=64)[:, :, 63:64].bc([128, 8, 64]), ALU.add)
                         cum = sg
                     kb.act(e1.v, cum.v, AF.Exp)
                     kb.act(e2.v, cum.v, AF.Exp, scale=-1.0)
                     return kk, e1, e2

                 def halo(kind):
                     d = 0 if kind == "f" else 1
                     src = xhf_d if d == 0 else xhb_d
                     ln_tile(lambda sub: src[sub * 128:(sub + 1) * 128, :],
                             M_SCLHF if d == 0 else M_SCLHB, M_SHFHF if d == 0 else M_SHFHB)
                     chk(11)
                     items = []
                     for h in range(NH):
                         items.append((wfm[h], KC * 384, ("p (k c) -> p k c", dict(k=KC))))
                         items.append((wtm[h], KC * 256, ("p (k c) -> p k c", dict(k=KC))))
                     rg = Ring(items, hold=2)
                     kb.memset("dve", S32h.v, 0.0)
                     kb.memset("pool", SBh.v, 0.0)
                     hctx = {}

                     def hproj(h):
                         wf_ = rg.get(2 * h)[:, 0:KC * 384].r("p (k c) -> p k c", k=KC)
                         wt_ = rg.get(2 * h + 1)[:, 0:KC * 256].r("p (k c) -> p k c", k=KC)
                         vh = Vh[h % 2]
                         for sub in range(4):
                             ps = bank(4)
                             for kc in range(KC):
                                 kb.mm(ps[:, 0:128], hT[:, kc, sub * 128:(sub + 1) * 128], wt_[:, kc, 0:128],
                                       kc == 0, kc == KC - 1)
                                 if kc % 8 == 7:
                                     yield
                             kb.ts("dve", vh[:, sub, :], ps[:, 0:128], vm[:, d * 4 + sub:d * 4 + sub + 1], ALU.mult)
                         ps = bank(4)
                         for kc in range(KC):
                             kb.mm(ps.v, wf_[:, kc, d * 128:(d + 1) * 128], hT[:, kc, :], kc == 0, kc == KC - 1)
                             if kc % 8 == 7:
                                 yield
                         hctx[h] = ps

                     def hprep(h):
                         kk, e1, e2 = gates(hctx[h].v, d, h, h % 2)
                         kt = Kt_[h % 4]
                         kb.tt("pool", kt.v, kk.v, e2.v, ALU.mult)
                         ee = e1.v.r("p (c k) -> p c k", k=64)
                         eend = ee[:, :, 63] if d == 0 else ee[:, :, 0]
                         kb.copy("dve", Eb_[h % 2].v, eend)

                     drain(hproj(0))
                     hprep(0)
                     for h in range(NH):
                         fil = hproj(h + 1) if h + 1 < NH else None
                         sc = scan_dir(s1, d == 0, None, Kt_[h % 4].v, Vh[h % 2].v, Eb_[h % 2].v, S32h[:, h, :],
                                       SBh[:, h, :], tb, None, None)
                         fil = interleave(sc, fil)
                         drain(fil)
                         if h + 1 < NH:
                             hprep(h + 1)
                     kb.copy("dve", sstart[:, d, :, :], S32h.v)

                 halo("f")
                 chk(12)
                 halo("b")
                 chk(13)
                 kb.copy("dve", S32f.v, sstart[:, 0, :, :])
                 kb.copy("act", SBf.v, sstart[:, 0, :, :])
                 chk(131)

                 for t in range(NT):
                     ln_tile(lambda sub: x_d[t * T + sub * 128:t * T + (sub + 1) * 128, :], M_SCLA, M_SHFA)
                     chk(132)
                     items = []
                     for h in range(NH):
                         items.append((wfm[h], KC * 384, ("p (k c) -> p k c", dict(k=KC))))
                         items.append((wtm[h], KC * 256, ("p (k c) -> p k c", dict(k=KC))))
                     for c in range(NH):
                         items.append((wfm[8 + c], KC * 384, ("p (k c) -> p k c", dict(k=KC))))
                     rg = Ring(items, hold=2)
                     pctx = {}
                     qpad = lambda b_: b_.v.r("p (q a c) -> p q a c", q=4, a=3)[:, :, 0:3:2, :]
                     q4 = lambda v_: v_.r("p (q a c) -> p q a c", q=4, a=2)

                     def proj_head(h):
                         wf_ = rg.get(2 * h)[:, 0:KC * 384].r("p (k c) -> p k c", k=KC)
                         wt_ = rg.get(2 * h + 1)[:, 0:KC * 256].r("p (k c) -> p k c", k=KC)
                         vh, sgh = Vh[h % 3], SGh[h % 3]
                         for sub in range(4):
                             ps = bank(4)
                             for kc in range(KC):
                                 kb.mm(ps[:, 0:256], hT[:, kc, sub * 128:(sub + 1) * 128], wt_[:, kc, :],
                                       kc == 0, kc == KC - 1)
                                 if kc % 8 == 7:
                                     yield
                             kb.copy("dve", vh[:, sub, :], ps[:, 0:128])
                             kb.act(sgh[:, sub, :], ps[:, 128:256], AF.Silu)
                             kb.tt("pool", sgh[:, sub, :], sgh[:, sub, :], gwrep.v, ALU.mult)
                         pf = [bank(4) for _ in range(3)]
                         for g in range(3):
                             for kc in range(KC):
                                 kb.mm(pf[g].v, wf_[:, kc, g * 128:(g + 1) * 128], hT[:, kc, :], kc == 0, kc == KC - 1)
                                 if kc % 8 == 7:
                                     yield
                         pctx[h] = pf

                     def prep(h):
                         pf = pctx[h]
                         kb.act(qs.v, pf[2].v, AF.Silu)
                         kk, e1, e2 = gates(pf[0].v, 0, h, 0)
                         qpf, ktf = Qp_[h % 2], Kt_[h % 2]
                         kb.tt("dve", qpad(qpf), q4(qs.v), q4(e1.v), ALU.mult)
                         kb.tt("pool", ktf.v, kk.v, e2.v, ALU.mult)
                         kb.copy("dve", Ebf[h % 2].v, e1.v.r("p (c k) -> p c k", k=64)[:, :, 63])
                         kk2, e1b, e2b = gates(pf[1].v, 1, h, 1)
                         qpb, ktb = Qp_[2 + h % 2], Kt_[2 + h % 2]
                         kb.tt("dve", qpad(qpb), q4(qs.v), q4(e1b.v), ALU.mult)
                         kb.tt("pool", ktb.v, kk2.v, e2b.v, ALU.mult)
                         kb.copy("dve", Eb_[1].v, e1b.v.r("p (c k) -> p c k", k=64)[:, :, 0])
                         kb.dma("pool", s_qb[t, h], qpb.v)
                         kb.dma("pool", s_kb[t, h], ktb.v)
                         kb.dma("pool", s_eb[t, h], Eb_[1].v)

                     def conv_all():
                         for c in range(NH):
                             wf_ = rg.get(2 * NH + c)[:, 0:KC * 384].r("p (k c) -> p k c", k=KC)
                             pf = [bank(4) for _ in range(3)]
                             for g in range(3):
                                 for kc in range(KC):
                                     kb.mm(pf[g].v, wf_[:, kc, g * 128:(g + 1) * 128], hT[:, kc, :],
                                           kc == 0, kc == KC - 1)
                                     if kc % 8 == 7:
                                         yield
                             kb.copy("act", xv_s.v, pf[2].v)
                             kb.tt("dve", zc.v, pf[1].v, xv_s.v, ALU.mult)
                             z3 = zc.v.r("p (s k) -> p s k", k=64)
                             a3 = acc.v.r("p (s k) -> p s k", k=64)
                             kb.act(acc.v, zc.v, AF.Copy, scale=convw[:, 1, c:c + 1])
                             kb.stt(a3[:, :, 1:64], z3[:, :, 0:63], convw[:, 0, c:c + 1], a3[:, :, 1:64],
                                    ALU.mult, ALU.add)
                             kb.stt(a3[:, :, 0:63], z3[:, :, 1:64], convw[:, 2, c:c + 1], a3[:, :, 0:63],
                                    ALU.mult, ALU.add)
                             kb.tt("dve", yc[c % 2].v, pf[0].v, acc.v, ALU.mult)
                             kb.dma("pool", s_yc[t, c], yc[c % 2].v)

                     drain(proj_head(0))
                     prep(0)
                     drain(proj_head(1))
                     cgen = conv_all()
                     for h in range(NH):
                         vh, sgh, oh = Vh[h % 3], SGh[h % 3], Oh[h % 2]
                         if h + 1 < NH:
                             prep(h + 1)
                         fil = proj_head(h + 2) if h + 2 < NH else cgen
                         sc = scan_dir(s1, True, Qp_[h % 2].v, Kt_[h % 2].v, vh.v, Ebf[h % 2].v, S32f[:, h, :],
                                       SBf[:, h, :], tb, oh.v, None)
                         fil = interleave(sc, fil, first=5)
                         kb.dma("pool", s_v[t, h], vh.v.r("p a c -> p (a c)"))
                         kb.dma("pool", s_sg[t, h], sgh.v.r("p a c -> p (a c)"))
                         kb.dma("pool", s_of[t, h], oh.v.r("p a c -> p (a c)"))
                         if h + 2 < NH:
                             drain(fil)
                     drain(cgen)
                 kb.barrier()
        except StopBuild:
            kb.barrier()

        if _DBG.get("stop", 0) in (0, 2) or _DBG.get("stop", 0) >= 20:
         with ExitStack() as s2:
          if True:
                g1rep = kb.sbuf(s2, "g1rep", [128, D], F32)
                b1rep = kb.sbuf(s2, "b1rep", [128, D], F32)
                stat = kb.sbuf(s2, "stat2", [128, 32], F32)
                epst = kb.sbuf(s2, "epst2", [128, 1], F32)
                rst = kb.sbuf(s2, "rst", [128, 16], F32)
                junk = kb.sbuf(s2, "junk", [128, 128], F32)
                S32b = kb.sbuf(s2, "S32b", [128, NH, 128], F32)
                SBb = kb.sbuf(s2, "SBb", [128, NH, 128], BF16)
                qb = [kb.sbuf(s2, "qb%d" % i, [128, 768], BF16) for i in range(2)]
                kbb = [kb.sbuf(s2, "kbb%d" % i, [128, T], BF16) for i in range(2)]
                ebb = [kb.sbuf(s2, "ebb%d" % i, [128, 8], F32) for i in range(2)]
                vb = [kb.sbuf(s2, "vb%d" % i, [128, 4, 128], BF16) for i in range(2)]
                ofb = [kb.sbuf(s2, "ofb%d" % i, [128, 4, 128], F32) for i in range(2)]
                sgb = [kb.sbuf(s2, "sgb%d" % i, [128, 4, 128], F32) for i in range(2)]
                Yt = kb.sbuf(s2, "Yt", [128, 4, 1024], BF16)
                Yf = kb.sbuf(s2, "Yf", [128, KC, T], BF16)
                R1 = [kb.sbuf(s2, "R1_%d" % i, [128, D], F32) for i in range(4)]
                xn2 = [kb.sbuf(s2, "xn2%d" % i, [128, D], BF16) for i in range(2)]
                H2 = kb.sbuf(s2, "H2", [128, KC, T], BF16)
                tb = scan_tmps(s2)
                tb2 = scan_tmps(s2)
                kb.memset("dve", epst.v, EPS)
                kb.dma("sp", g1rep.v, ln1g_d[0:1, :].bc([128, D]))
                kb.dma("sp", b1rep.v, ln1b_d[0:1, :].bc([128, D]))
                kb.copy("dve", S32b.v, sstart[:, 1, :, :])
                kb.copy("act", SBb.v, sstart[:, 1, :, :])
                for t in range(NT - 1, -1, -1):
                    for hp in range(0, NH, 2):
                        gens = []
                        for i2 in range(2):
                            h = hp + i2
                            kb.dma("sp", qb[i2].v, s_qb[t, h])
                            kb.dma("sp", kbb[i2].v, s_kb[t, h])
                            kb.dma("sp", ebb[i2].v, s_eb[t, h])
                            kb.dma("sp", vb[i2].v.r("p a c -> p (a c)"), s_v[t, h])
                            kb.dma("sp", ofb[i2].v.r("p a c -> p (a c)"), s_of[t, h])
                            kb.dma("sp", sgb[i2].v.r("p a c -> p (a c)"), s_sg[t, h])
                            gens.append(scan_dir(s2, False, qb[i2].v, kbb[i2].v, vb[i2].v, ebb[i2].v, S32b[:, h, :],
                                                 SBb[:, h, :], tb if i2 == 0 else tb2, ofb[i2].v, ofb[i2].v, bset=i2))
                        rest = interleave(gens[0], gens[1])
                        drain(rest)
                        for i2 in range(2):
                            h = hp + i2
                            for sub in range(4):
                                kb.act(junk.v, ofb[i2][:, sub, :], AF.Square, accum=rst[:, sub:sub + 1])
                            kb.act(rst[:, 4:8], rst[:, 0:4], AF.Sqrt, scale=1.0 / 128.0, bias=epst[:, 0:1])
                            kb.op("dve", lambda e: e.reciprocal(out=rst.t[:, 8:12], in_=rst.t[:, 4:8]),
                                  reads=[rst], writes=[rst])
                            for sub in range(4):
                                kb.stt(Yt[:, sub, h * 128:(h + 1) * 128], ofb[i2][:, sub, :], rst[:, 8 + sub:9 + sub],
                                       sgb[i2][:, sub, :], ALU.mult, ALU.mult)
                    for sub in range(4):
                        ps = bank(4)
                        pv = bfv(ps)
                        for h in range(NH):
                            kb.tr(pv[:, h * 128:(h + 1) * 128], Yt[:, sub, h * 128:(h + 1) * 128], ident.v)
                        kb.copy("act" if sub % 2 else "dve", Yf[:, 0:8, sub * 128:(sub + 1) * 128],
                                pv.r("p (h c) -> p h c", h=8))
                    kb.dma("sp", Yf[:, 8:16, :], s_yc[t].r("c p n -> p c n"))
                    for sub in range(4):
                        kb.dma("sp", R1[sub].v, x_d[t * T + sub * 128:t * T + (sub + 1) * 128, :])
                    rg = Ring([(wo[g], KC * 512, ("p (k c) -> p k c", dict(k=KC))) for g in range(4)])
                    for g in range(4):
                        w_ = rg.get(g)[:, 0:KC * 512].r("p (k c) -> p k c", k=KC)
                        for sub in range(4):
                            ps = bank(4)
                            for kc in range(KC):
                                kb.mm(ps.v, Yf[:, kc, sub * 128:(sub + 1) * 128], w_[:, kc, :], kc == 0, kc == KC - 1)
                            kb.stt(R1[sub][:, g * 512:(g + 1) * 512], R1[sub][:, g * 512:(g + 1) * 512], ALPHA, ps.v,
                                   ALU.mult, ALU.add)
                    for sub in range(4):
                        r1 = R1[sub].v
                        ln_stats(r1, stat, epst)
                        kb.act(r1, r1, AF.Identity, scale=stat[:, 0:1], bias=stat[:, 1:2])
                        kb.tt("dve", r1, r1, g1rep.v, ALU.mult)
                        kb.tt("pool", r1, r1, b1rep.v, ALU.add)
                        kb.dma("pool", s_x1[t, sub], r1)
                        ln_stats(r1, stat, epst)
                        kb.act(xn2[sub % 2].v, r1, AF.Identity, scale=stat[:, 0:1], bias=stat[:, 1:2])
                        to_fm(xn2[sub % 2], H2, sub * 128, M_SCLF, M_SHFF)
                    kb.dma("pool", s_h2[t], H2.v)
                kb.barrier()

        if _DBG.get("stop", 0) == 0 or _DBG.get("stop", 0) >= 30:
         with ExitStack() as s3:
          if True:
                g2rep = kb.sbuf(s3, "g2rep", [128, D], F32)
                b2rep = kb.sbuf(s3, "b2rep", [128, D], F32)
                stat = kb.sbuf(s3, "stat3", [128, 32], F32)
                epst = kb.sbuf(s3, "epst3", [128, 1], F32)
                H2b = [kb.sbuf(s3, "H2b0", [128, KC, T], BF16)]
                A = kb.sbuf(s3, "A", [128, JF, T], BF16)
                sgt = [kb.sbuf(s3, "sgt%d" % i, [128, T], F32) for i in range(2)]
                R2 = [kb.sbuf(s3, "R2_%d" % i, [128, D], F32) for i in range(4)]
                kb.memset("dve", epst.v, EPS)
                kb.dma("sp", g2rep.v, ln2g_d[0:1, :].bc([128, D]))
                kb.dma("sp", b2rep.v, ln2b_d[0:1, :].bc([128, D]))
                for t in range(NT):
                    h2 = H2b[0]
                    kb.dma("sp", h2.v, s_h2[t])
                    items = [(wgu[jg], KC * 512, ("p (k c) -> p k c", dict(k=KC))) for jg in range(22)]
                    for g in range(4):
                        for pc in range(4):
                            items.append((wdn[g][:, pc * 11:(pc + 1) * 11, :], 11 * 512, ("p (j c) -> p j c", dict(j=11))))
                    rg = Ring(items)
                    for jg in range(22):
                        w_ = rg.get(jg)[:, 0:KC * 512].r("p (k jj gu c) -> p k jj gu c", k=KC, jj=2, gu=2)
                        for jj in range(2):
                            j = jg * 2 + jj
                            pg, pu = bank(8), bank(8)
                            for kc in range(KC):
                                kb.mm(pg.v, w_[:, kc, jj, 0, :], h2[:, kc, :], kc == 0, kc == KC - 1)
                            for kc in range(KC):
                                kb.mm(pu.v, w_[:, kc, jj, 1, :], h2[:, kc, :], kc == 0, kc == KC - 1)
                            kb.act(sgt[j % 2].v, pg.v, AF.Silu)
                            kb.tt("dve", A[:, j, :], pu.v, sgt[j % 2].v, ALU.mult)
                    for sub in range(4):
                        kb.dma("sp", R2[sub].v, s_x1[t, sub])
                    for g in range(4):
                        pss = [bank(8) for _ in range(4)]
                        for pc in range(4):
                            w_ = rg.get(22 + g * 4 + pc)[:, 0:11 * 512].r("p (j c) -> p j c", j=11)
                            for sub in range(4):
                                for jj in range(11):
                                    j = pc * 11 + jj
                                    kb.mm(pss[sub].v, A[:, j, sub * 128:(sub + 1) * 128], w_[:, jj, :], j == 0, j == JF - 1,
                                          signal=(jj == 10))
                        for sub in range(4):
                            kb.stt(R2[sub][:, g * 512:(g + 1) * 512], R2[sub][:, g * 512:(g + 1) * 512], ALPHA,
                                   pss[sub].v, ALU.mult, ALU.add)
                    for sub in range(4):
                        r2 = R2[sub].v
                        ln_stats(r2, stat, epst)
                        kb.act(r2, r2, AF.Identity, scale=stat[:, 0:1], bias=stat[:, 1:2])
                        kb.tt("dve", r2, r2, g2rep.v, ALU.mult)
                        kb.tt("pool", r2, r2, b2rep.v, ALU.add)
                        kb.dma("pool", out_d[t * T + sub * 128:t * T + (sub + 1) * 128, :], r2)
                kb.barrier()
        print("kernel: %d instructions emitted" % kb.ninst)
    _DBG["names"] = kb.names
    return nc


def make_in_maps(inp, NT):
    x = np.asarray(inp["x"], np.float32)
    B, N, _ = x.shape
    seg = NT * T
    nseg = N // seg
    ctx = np.asarray(inp["ctx"], np.float32)
    c = np.asarray(inp["c"], np.float32)
    c_ctx = np.asarray(inp["c_ctx"], np.float32)
    f32 = lambda a: np.ascontiguousarray(np.asarray(a, np.float32))
    lbl = f32(np.asarray(inp["lb_logits"]).reshape(2, 2, NH, 128).transpose(3, 0, 1, 2))
    convw = f32(np.asarray(inp["conv_w"])[0].reshape(3, NH, 128).transpose(2, 0, 1))
    bmod = f32(np.asarray(inp["b_mod"])[0])
    shared = {
        "w_mod": f32(inp["w_mod"][0]), "bmod_fm": f32(bmod.reshape(96, 128).T), "b_mod": f32(bmod[None, :]),
        "w_in": f32(inp["w_in"][0]), "lbl": lbl, "g_norm_w": f32(np.asarray(inp["g_norm_w"])[0][None, :]),
        "convw": convw, "w_out": f32(inp["w_out"][0]),
        "ln1_g": f32(np.asarray(inp["ln1_g"])[0][None, :]), "ln1_b": f32(np.asarray(inp["ln1_b"])[0][None, :]),
        "w_gate": f32(inp["w_gate"][0]), "w_up": f32(inp["w_up"][0]), "w_down": f32(inp["w_down"][0]),
        "ln2_g": f32(np.asarray(inp["ln2_g"])[0][None, :]), "ln2_b": f32(np.asarray(inp["ln2_b"])[0][None, :]),
    }
    maps = []
    for r in range(B * nseg):
        b, j = divmod(r, nseg)
        t0 = j * seg
        m = dict(shared)
        m["x"] = f32(x[b, t0:t0 + seg])
        xhf = np.zeros((T, D), np.float32)
        xhb = np.zeros((T, D), np.float32)
        vmf = np.ones(T, np.float32)
        vmb = np.ones(T, np.float32)
        sel = np.zeros((128, 2), np.float32)
        if j == 0:
            xhf[T - CTX:] = ctx[b]
            vmf[:T - CTX] = 0.0
            sel[:, 0] = 1.0
        else:
            xhf[:] = x[b, t0 - T:t0]
        if j == nseg - 1:
            xhb[:CTX] = ctx[b]
            vmb[CTX:] = 0.0
            sel[:, 1] = 1.0
        else:
            xhb[:] = x[b, t0 + seg:t0 + seg + T]
        m["xhf"], m["xhb"], m["sel"] = xhf, xhb, sel
        m["vm"] = f32(np.concatenate([vmf.reshape(4, 128).T, vmb.reshape(4, 128).T], axis=1))
        cv = np.stack([c[b].reshape(KC, 128).T, c_ctx.reshape(KC, 128).T], axis=2)
        m["cvec"] = f32(cv)
        maps.append(m)
    return maps, B, nseg, seg


_NC_CACHE = {}


def kernel(**inputs):
    x = np.asarray(inputs["x"])
    B, N, _ = x.shape
    NT = N // (4 * T)
    maps, B, nseg, seg = make_in_maps(inputs, NT)
    if NT not in _NC_CACHE:
        _NC_CACHE[NT] = build_program(NT)
    nc = _NC_CACHE[NT]
    res = run_bass_kernel_spmd(nc, maps, core_ids=list(range(len(maps))))
    out = np.empty((B, N, D), np.float32)
    for r in range(B * nseg):
        b, j = divmod(r, nseg)
        out[b, j * seg:(j + 1) * seg] = res.results[r]["out"]
    return out
```
